# Optimizing a Trainium2 kernel written in Bass

```python
import jax, jax.numpy as jnp
from jax import lax
import numpy as np

D_MODEL = 1024
BATCH = 16
SEQ = 4096
DEPTH = 2
DEC_BATCH = 8
DEC_SEQ = 64
PAST_LEN = 4096

CHUNK = 64
D_MIX = D_MODEL
POOL_WIDTH = D_MIX // 2
POOL_GROUPS = 4
POOL_GROUP_DIM = POOL_WIDTH // POOL_GROUPS
POOL_WINDOWS = (2, 4, 8, 16)
POOL_KEEP = max(POOL_WINDOWS) - 1
SGU_WIDTH = D_MIX - POOL_WIDTH
SGU_HEADS = 4
SGU_HEAD_DIM = SGU_WIDTH // SGU_HEADS
SGU_CHUNK = 128
D_IN = POOL_WIDTH + 2 * SGU_WIDTH
D_FF = 4 * D_MODEL
N_MOD = 6
EPS = 1e-6

kernel_name = "hybrid_pool_sgu_streaming_step"


def rms_norm(x, g):
    xf = x.astype(jnp.float32)
    y = xf * lax.rsqrt(jnp.mean(xf * xf, axis=-1, keepdims=True) + EPS)
    return (y * g.astype(jnp.float32)).astype(x.dtype)


def layer_norm(x, g, b):
    xf = x.astype(jnp.float32)
    mu = jnp.mean(xf, axis=-1, keepdims=True)
    var = jnp.mean(jnp.square(xf - mu), axis=-1, keepdims=True)
    y = (xf - mu) * lax.rsqrt(var + EPS)
    return (y * g.astype(jnp.float32) + b.astype(jnp.float32)).astype(x.dtype)


def pool_mixer(a, prev, pos0, w_pool, scale):
    B, L, _ = a.shape
    xp = jnp.concatenate([prev.astype(a.dtype), a], axis=1)
    xf = xp.astype(jnp.float32)
    csum = jnp.concatenate([jnp.zeros((B, 1, POOL_WIDTH), jnp.float32),
                            jnp.cumsum(xf, axis=1)], axis=1)
    pos = pos0 + jnp.arange(L, dtype=jnp.int32)
    means = []
    for g, w in enumerate(POOL_WINDOWS):
        sl = slice(g * POOL_GROUP_DIM, (g + 1) * POOL_GROUP_DIM)
        hi = csum[:, POOL_KEEP + 1:, sl]
        lo = csum[:, POOL_KEEP + 1 - w:POOL_KEEP + 1 - w + L, sl]
        cnt = jnp.minimum(pos + 1, w).astype(jnp.float32)
        means.append((hi - lo) / cnt[None, :, None])
    pooled = jnp.concatenate(means, axis=-1)
    d = (pooled - a.astype(jnp.float32)).astype(a.dtype)
    d = d.reshape(B, L, POOL_GROUPS, POOL_GROUP_DIM)
    out = jnp.einsum('blgc,gcd->blgd', d, w_pool).reshape(B, L, POOL_WIDTH) * scale
    new_prev = xp[:, -POOL_KEEP:]
    return out, new_prev


def spatial_gating(u, v, w_s, b_s):
    B, L, _ = v.shape
    T = min(L, SGU_CHUNK)
    n = L // T
    mask = jnp.tril(jnp.ones((T, T), dtype=bool))
    w = jnp.where(mask[None], w_s[:, :T, :T], jnp.zeros((), w_s.dtype))
    vh = v.reshape(B, n, T, SGU_HEADS, SGU_HEAD_DIM)
    mixed = jnp.einsum('hts,bnshc->bnthc', w, vh) + b_s[:, :T].T[None, None, :, :, None]
    return u * mixed.reshape(B, L, SGU_WIDTH)


def trunk_layer(x, c, prev_pool, pos0, w_ada, b_ada, norm_mix_g, w_in, w_pool, pool_scale,
                v_norm_g, v_norm_b, w_spatial, b_spatial, w_out, norm_ffn_g, w_ff1, w_ff2):
    mod = jax.nn.silu(c) @ w_ada + b_ada
    sh1, sc1, g1, sh2, sc2, g2 = jnp.split(mod, N_MOD, axis=-1)
    h = rms_norm(x, norm_mix_g) * (1.0 + sc1[:, None]) + sh1[:, None]
    z = h @ w_in
    a = z[..., :POOL_WIDTH]
    u = jax.nn.gelu(z[..., POOL_WIDTH:POOL_WIDTH + SGU_WIDTH], approximate=False)
    v = layer_norm(jax.nn.gelu(z[..., POOL_WIDTH + SGU_WIDTH:], approximate=False), v_norm_g, v_norm_b)
    ya, new_prev = pool_mixer(a, prev_pool, pos0, w_pool, pool_scale)
    yb = spatial_gating(u, v, w_spatial, b_spatial)
    x = x + g1[:, None] * (jnp.concatenate([ya, yb], axis=-1) @ w_out)
    h2 = rms_norm(x, norm_ffn_g) * (1.0 + sc2[:, None]) + sh2[:, None]
    f = jnp.square(jax.nn.relu(h2 @ w_ff1)) @ w_ff2
    x = x + g2[:, None] * f
    return x, new_prev, v


def setup_inputs(seed: int = 0) -> dict:
    key = jax.random.key(seed)
    ks = jax.random.split(key, 20)
    f32 = jnp.float32
    nrm = lambda k, shape: jax.random.normal(k, shape, f32)
    return {
        "x_prompt": nrm(ks[0], (BATCH, SEQ, D_MODEL)),
        "x_sample": nrm(ks[1], (DEC_BATCH, DEC_SEQ, D_MODEL)),
        "state_pool": nrm(ks[2], (DEPTH, DEC_BATCH, POOL_KEEP, POOL_WIDTH)),
        "c_prompt": nrm(ks[3], (BATCH, D_MODEL)),
        "c_sample": nrm(ks[4], (DEC_BATCH, D_MODEL)),
        "w_ada": nrm(ks[5], (DEPTH, D_MODEL, N_MOD * D_MODEL)) * (0.5 * D_MODEL ** -0.5),
        "b_ada": nrm(ks[6], (DEPTH, N_MOD * D_MODEL)) * 0.02,
        "norm_mix_g": 1.0 + 0.05 * nrm(ks[7], (DEPTH, D_MODEL)),
        "w_in": nrm(ks[8], (DEPTH, D_MODEL, D_IN)) * D_MODEL ** -0.5,
        "w_pool": nrm(ks[9], (DEPTH, POOL_GROUPS, POOL_GROUP_DIM, POOL_GROUP_DIM)) * POOL_GROUP_DIM ** -0.5,
        "pool_scale": 1.0 + 0.1 * nrm(ks[10], (DEPTH, POOL_WIDTH)),
        "v_norm_g": 1.0 + 0.05 * nrm(ks[11], (DEPTH, SGU_WIDTH)),
        "v_norm_b": 0.02 * nrm(ks[12], (DEPTH, SGU_WIDTH)),
        "w_spatial": nrm(ks[13], (DEPTH, SGU_HEADS, SGU_CHUNK, SGU_CHUNK)) * SGU_CHUNK ** -0.5,
        "b_spatial": 1.0 + 0.1 * nrm(ks[14], (DEPTH, SGU_HEADS, SGU_CHUNK)),
        "w_out": nrm(ks[15], (DEPTH, D_MIX, D_MODEL)) * D_MIX ** -0.5,
        "norm_ffn_g": 1.0 + 0.05 * nrm(ks[16], (DEPTH, D_MODEL)),
        "w_ff1": nrm(ks[17], (DEPTH, D_MODEL, D_FF)) * D_MODEL ** -0.5,
        "w_ff2": nrm(ks[18], (DEPTH, D_FF, D_MODEL)) * D_FF ** -0.5,
        "final_norm_g": 1.0 + 0.05 * nrm(ks[19], (D_MODEL,)),
    }


def reference(x_prompt, x_sample, state_pool, c_prompt, c_sample, w_ada, b_ada, norm_mix_g, w_in,
              w_pool, pool_scale, v_norm_g, v_norm_b, w_spatial, b_spatial, w_out, norm_ffn_g,
              w_ff1, w_ff2, final_norm_g):
    xp, xs = x_prompt, x_sample
    pool_p, pool_s, v_s = [], [], []
    for l in range(DEPTH):
        lw = (w_ada[l], b_ada[l], norm_mix_g[l], w_in[l], w_pool[l], pool_scale[l], v_norm_g[l],
              v_norm_b[l], w_spatial[l], b_spatial[l], w_out[l], norm_ffn_g[l], w_ff1[l], w_ff2[l])
        prev0 = jnp.zeros((xp.shape[0], POOL_KEEP, POOL_WIDTH), xp.dtype)
        xp, np_p, _ = trunk_layer(xp, c_prompt, prev0, 0, *lw)
        xs, np_s, vs = trunk_layer(xs, c_sample, state_pool[l], PAST_LEN, *lw)
        pool_p.append(np_p)
        pool_s.append(np_s)
        v_s.append(vs)
    y_prompt = rms_norm(xp, final_norm_g)
    y_sample = rms_norm(xs, final_norm_g)
    state_pool_prompt = jnp.stack(pool_p, axis=0)
    state_pool_sample = jnp.stack(pool_s, axis=0)
    state_sgu_v_sample = jnp.stack(v_s, axis=0)
    return (y_prompt, y_sample, state_pool_prompt, state_pool_sample, state_sgu_v_sample)
```

```python
import os
import sys
import numpy as np
from contextlib import ExitStack
import concourse.bass as bass
import concourse.mybir as mybir
from concourse.bass_utils import run_bass_kernel_spmd

F32 = mybir.dt.float32
BF16 = mybir.dt.bfloat16
AF = mybir.ActivationFunctionType
ALU = mybir.AluOpType

NCORES = 8
D = 1024
KD = 8
DEPTH = 2
SEQ = 4096
DEC = 64
NTOK = 2 * SEQ + DEC
TILE = 1024
TILEW = TILE + 64
SUB = 512
EPS = 1e-6
RING = 5
NTF = 5
NBANK = 7
POOL_W = (2, 4, 8, 16)

C_C = 0
C_BADA = 24
C_GMIX = 120
C_GFFN = 136
C_GFIN = 152
C_PSC = 160
C_GV = 168
C_BV = 176
C_INV = 184
NSMALL = 248


_KDEBUG = bool(os.environ.get("KDEBUG"))
_KTAGS = {}


class Op:
    __slots__ = ("eng", "fn", "deps", "signal", "sem", "token", "idx", "is_dma", "tag")

    def __init__(self, eng, fn, idx, sem=None):
        self.eng = eng
        self.fn = fn
        self.deps = []
        self.signal = False
        self.sem = sem
        self.token = None
        self.idx = idx
        self.is_dma = sem is not None


class Sched:
    ENGS = ("pe", "act", "dve", "pool", "sp")

    def __init__(self):
        self.ops = []
        self.last_w = {}
        self.readers = {}
        self.total_wait = set()

    def op(self, eng, fn, reads=(), writes=(), after=(), sem=None):
        o = Op(eng, fn, len(self.ops), sem)
        o.tag = None
        if _KDEBUG:
            f = sys._getframe(1)
            names = []
            while f is not None and len(names) < 6:
                nm = f.f_code.co_name
                if nm not in ("OP", "act", "tt", "ts", "stt", "cp", "mm", "<lambda>"):
                    names.append("%s:%d" % (nm, f.f_lineno))
                f = f.f_back
            o.tag = "|".join(names[:3])
        deps = {}
        for k in reads:
            w = self.last_w.get(k)
            if w is not None:
                deps[w.idx] = w
        for k in writes:
            w = self.last_w.get(k)
            if w is not None:
                deps[w.idx] = w
            for r in self.readers.get(k, {}).values():
                if isinstance(r, list):
                    for rr in r:
                        deps[rr.idx] = rr
                else:
                    deps[r.idx] = r
        for a in after:
            if a is not None:
                deps[a.idx] = a
        for k in reads:
            d = self.readers.setdefault(k, {})
            if o.is_dma:
                d.setdefault("dma", []).append(o)
            else:
                d[eng] = o
        for k in writes:
            self.last_w[k] = o
            self.readers[k] = {}
        o.deps = list(deps.values())
        for d in o.deps:
            d.signal = True
        self.ops.append(o)
        return o

    def emit(self, nc):
        semkeys = list(self.ENGS[:4])
        for o in self.ops:
            if o.is_dma and o.sem not in semkeys:
                semkeys.append(o.sem)
        counters = {k: 0 for k in semkeys}
        for o in self.ops:
            if o.is_dma:
                counters[o.sem] += 16
                o.token = (o.sem, counters[o.sem])
            elif o.signal:
                counters[o.eng] += 1
                o.token = (o.eng, counters[o.eng])
        for o in self.ops:
            if o.is_dma and o.sem in self.total_wait:
                o.token = (o.sem, counters[o.sem])
        per_eng = {e: [o for o in self.ops if o.eng == e] for e in self.ENGS}
        with ExitStack() as es:
            sems = {k: es.enter_context(nc.semaphore("s_%d" % i)) for i, k in enumerate(semkeys)}
            block = es.enter_context(nc.Block())

            def run(eng_name):
                def body(h):
                    waited = {}
                    for o in per_eng[eng_name]:
                        for d in o.deps:
                            if (not d.is_dma) and d.eng == "pe" and eng_name == "pe":
                                continue
                            sk, val = d.token
                            if waited.get(sk, 0) >= val:
                                continue
                            waited[sk] = val
                            h.wait_ge(sems[sk], val)
                        ins = o.fn(h) if o.fn is not None else None
                        if _KDEBUG and ins is not None:
                            _KTAGS[str(ins.ins.name)] = o.tag
                        if o.is_dma:
                            ins.then_inc(sems[o.sem], 16)
                        elif o.signal:
                            ins.then_inc(sems[o.eng], 1)
                return body

            block.tensor(run("pe"))
            block.scalar(run("act"))
            block.vector(run("dve"))
            block.gpsimd(run("pool"))
            block.sync(run("sp"))


def build_nc():
    nc = bass.Bass("TRN2", target_bir_lowering=False)
    dt_in = lambda n, s: nc.dram_tensor(n, s, F32, kind="ExternalInput").ap()
    dt_out = lambda n, s: nc.dram_tensor(n, s, F32, kind="ExternalOutput").ap()
    x_fm = dt_in("x_fm", [128, KD, NTOK])
    smallp_d = dt_in("smallp", [128, NSMALL])
    bsrow_d = dt_in("bsrow", [DEPTH, 512])
    gbrow_d = dt_in("gbrow", [DEPTH, 1024])
    maskt_d = dt_in("maskt", [128, 128])
    wspT_d = dt_in("wspT", [128, DEPTH * 4 * 128])
    wpool_d = dt_in("wpool_l", [128, DEPTH * 4 * 128])
    statein_d = dt_in("statein", [DEPTH, 15, 512])
    dwin_d = dt_in("dwin", [128, 3 * 4 * 128 + 4 * 64])
    w_ada = dt_in("w_ada", [DEPTH, D, 6 * D])
    w_in = dt_in("w_in", [DEPTH, D, 1536])
    w_out = dt_in("w_out", [DEPTH, D, D])
    w_ff1 = dt_in("w_ff1", [DEPTH, D, 4 * D])
    w_ff2 = dt_in("w_ff2", [DEPTH, 4 * D, D])
    y_fm = dt_out("y_fm", [128, KD, NTOK])
    pool_out = dt_out("pool_out", [3, DEPTH, 15, 512])
    vout = dt_out("vout", [DEPTH, DEC, 512])

    with ExitStack() as es:
        sb = lambda n, s, d: es.enter_context(nc.sbuf_tensor(n, s, d))
        xs = [sb("xs%d" % j, [128, KD, SUB], F32) for j in range(3)] + [sb("xs3", [128, KD, DEC], F32)]
        h = sb("h", [128, KD, TILEW], BF16)
        Z = sb("Z", [128, 16 * TILEW], BF16)
        hid = Z[:, :].rearrange("p (f t) -> p f t", f=16)
        yab = Z[:, 0:8 * TILEW].rearrange("p (k t) -> p k t", k=8)
        u_t = Z[:, 8 * TILEW:12 * TILEW].rearrange("p (k t) -> p k t", k=4)
        n_t = Z[:, 12 * TILEW:12 * TILEW + 2048].rearrange("p (c n) -> p c n", c=4)
        d_t = Z[:, 12 * TILEW + 2048:12 * TILEW + 4096].rearrange("p (g t) -> p g t", g=4)
        sq = [sb("sq%d" % j, [128, SUB], BF16) for j in range(KD)]
        rs = [sb("rs%d" % j, [128, SUB], F32) for j in range(2)]
        rs_fin = sb("rs_fin", [128, SUB], F32)
        sqc = [sb("sqc%d" % j, [128, DEC], BF16) for j in range(KD)]
        tf = [sb("tf%d" % j, [128, SUB], F32) for j in range(NTF)]
        vg = [sb("vg%d" % j, [128, 512], F32) for j in range(4)]
        atm = [sb("atm%d" % j, [128, 512], BF16) for j in range(4)]
        akeep = sb("akeep", [128, DEPTH, 512], BF16)
        statebf = sb("statebf", [128, DEPTH, 512], BF16)
        af32 = sb("af32", [128, 512], F32)
        Dw = sb("Dw", [128, 3 * 4 * 128 + 4 * 64], BF16)
        st6 = sb("st6", [128, 4, 6], F32)
        mv = sb("mv", [128, 4, 2], F32)
        sd = sb("sd", [128, 4], F32)
        d_t2 = sb("d_t2", [128, 4, SUB], BF16)
        ring = [sb("ring%d" % j, [128, 4096], BF16) for j in range(RING)]
        smallp = sb("smallp_sb", [128, NSMALL], F32)
        scb = sb("scb", [128, KD, 3], BF16)
        mod = sb("mod", [128, DEPTH, 48, 3], F32)
        gm1 = sb("gm1", [128, DEPTH, KD, 3], F32)
        gm2 = sb("gm2", [128, DEPTH, KD, 3], F32)
        WTraw = sb("WTraw", [128, DEPTH * 4, 128], BF16)
        WT = sb("WT", [128, DEPTH * 4, 128], BF16)
        maskb = sb("maskb", [128, 128], BF16)
        Bt = sb("Bt", [128, DEPTH * 4, 128], F32)
        wpool = sb("wpool", [128, DEPTH * 4, 128], BF16)
        ones = sb("ones", [128, 128], BF16)
        ps = [es.enter_context(nc.psum_tensor("ps%d" % i, [128, 512], F32)) for i in range(8)]

        def record(S, plan):
            collected = []
            cnt = {"bank": 0, "rs": 0, "tf": 0, "vg": 0}

            def rot(name, n):
                v = cnt[name] % n
                cnt[name] += 1
                return v

            zstate = {"prev": {}, "cur": {}}

            def zphase():
                zstate["prev"] = zstate["cur"]
                zstate["cur"] = {}

            def OP(eng, fn, reads=(), writes=(), z=False, after=(), sem=None):
                aft = list(after)
                if z:
                    aft += list(zstate["prev"].values())
                o = S.op(eng, fn, reads, writes, aft, sem)
                if z:
                    zstate["cur"][eng] = o
                return o

            def act(out, in_, func, reads, writes, bias=0.0, scale=1.0, z=False):
                return OP("act", lambda e: e.activation(out=out, in_=in_, func=func, bias=bias, scale=scale),
                          reads, writes, z)

            def tt(out, in0, in1, op, reads, writes, z=False, eng="dve"):
                return OP(eng, lambda e: e.tensor_tensor(out=out, in0=in0, in1=in1, op=op), reads, writes, z)

            def ts(out, in0, s1, s2, op0, op1, reads, writes, z=False, eng="dve"):
                if op1 is None:
                    return OP(eng, lambda e: e.tensor_scalar(out, in0, s1, None, op0), reads, writes, z)
                return OP(eng, lambda e: e.tensor_scalar(out, in0, s1, s2, op0, op1), reads, writes, z)

            def stt(out, in0, scalar, in1, op0, op1, reads, writes, z=False, eng="dve"):
                return OP(eng, lambda e: e.scalar_tensor_tensor(out=out, in0=in0, scalar=scalar, in1=in1,
                                                                op0=op0, op1=op1), reads, writes, z)

            def cp(out, in_, reads, writes, z=False, eng="dve"):
                return OP(eng, lambda e: e.tensor_copy(out=out, in_=in_), reads, writes, z)

            def mm(mms, reads, writes, z=False):
                def fn(e):
                    ins = None
                    for (o_, l_, r_, st_, sp_) in mms:
                        ins = e.matmul(o_, l_, r_, start=st_, stop=sp_)
                    return ins
                return OP("pe", fn, reads, writes, z)

            def col(c):
                return smallp[:, c:c + 1]

            nbk_box = [lambda: NBANK]

            def nbk():
                return nbk_box[0]()

            bstate = {"n": 0, "next_load": 0}
            slot_idx = {}

            def rec_load(i, src, kk):
                slot = i % RING
                dst = ring[slot][:, :].rearrange("p (k n) -> p k n", k=kk)
                S.op("pool", lambda e: e.dma_start(out=dst, in_=src), writes=[("ring", slot)], sem=("ring", slot))

            def acquire(src, kk):
                i = bstate["n"]
                bstate["n"] += 1
                slot = i % RING
                slot_idx[slot] = i
                if plan is None:
                    collected.append((src, kk))
                    rec_load(i, src, kk)
                else:
                    assert i < bstate["next_load"], "ring block used before its load was recorded"
                return slot, ring[slot][:, :].rearrange("p (k n) -> p k n", k=kk)

            def release(b):
                if plan is None:
                    return
                j = slot_idx[b[0]] + RING
                assert j == bstate["next_load"], "blocks must be released in acquisition order"
                if j < len(plan):
                    rec_load(j, *plan[j])
                bstate["next_load"] = j + 1

            def blk_in(l, i):
                return acquire(w_in[l, :, i * 512:(i + 1) * 512].rearrange("(k p) n -> p k n", p=128), 8)

            def blk_out(l, i):
                return acquire(w_out[l, :, i * 512:(i + 1) * 512].rearrange("(k p) n -> p k n", p=128), 8)

            def blk_ff1(l, half, fb):
                c0 = (half * 4 + fb) * 512
                return acquire(w_ff1[l, :, c0:c0 + 512].rearrange("(k p) n -> p k n", p=128), 8)

            def blk_ff2(l, half, mb):
                return acquire(w_ff2[l, half * 2048:(half + 1) * 2048, mb * 256:(mb + 1) * 256]
                               .rearrange("(f p) n -> p f n", p=128), 16)

            S.total_wait.add("setup")
            SPL = lambda out, in_, w: S.op("sp", lambda e: e.dma_start(out=out, in_=in_), writes=w, sem="setup")
            SPL(smallp[:, :], smallp_d[:, :], ["smallp"])
            S.total_wait.add("setupc")
            PLL = lambda out, in_, w: S.op("pool", lambda e: e.dma_start(out=out, in_=in_), writes=w, sem="setupc")
            PLL(WTraw[:, :, :], wspT_d.rearrange("p (a t) -> p a t", t=128), ["WTraw"])
            PLL(wpool[:, :, :], wpool_d.rearrange("p (a t) -> p a t", t=128), ["wpool"])
            PLL(maskb[:, :], maskt_d[:, :], ["maskb"])
            PLL(Dw[:, :], dwin_d[:, :], ["Dw"])
            PLL(statebf[0:15, :, :], statein_d.rearrange("l r c -> r l c"), ["statebf"])
            for l in range(DEPTH):
                SPL(tf[l][:, :], bsrow_d[l:l + 1, :].partition_broadcast(128), [("tf", l)])
            if plan is not None:
                for i0 in range(min(RING, len(plan))):
                    rec_load(i0, *plan[i0])
                bstate["next_load"] = min(RING, len(plan))
            OP("dve", lambda e: e.memset(ones[:, :], 1.0), writes=["ones"])
            tt(WT[:, :, :], WTraw[:, :, :], maskb[:, :].unsqueeze(1).to_broadcast([128, DEPTH * 4, 128]), ALU.mult,
               ["WTraw", "maskb"], ["WT"])
            act(scb[:, :, :], smallp[:, C_C:C_C + 24].rearrange("p (k b) -> p k b", b=3), AF.Silu, ["smallp"], ["scb"])
            MB = 7

            def mod_job(l, j, hb):
                c0 = j * 1024 + hb * 512
                slot, blk = acquire(w_ada[l, :, c0:c0 + 512].rearrange("(k p) n -> p k n", p=128), 8)
                mms = []
                for m in range(4):
                    cc = (j * 8 + hb * 4 + m) * 3
                    for k in range(KD):
                        mms.append((ps[MB][:, cc:cc + 3], blk[:, k, m * 128:(m + 1) * 128], scb[:, k, :],
                                    k == 0, k == KD - 1))
                mm(mms, [("ring", slot), "scb"], [("ps", MB)])
                release((slot, blk))

            def mod_finish_kind(l, j):
                tt(mod[:, l, j * 8:(j + 1) * 8, :], ps[MB][:, j * 24:(j + 1) * 24].rearrange("p (j b) -> p j b", b=3),
                   smallp[:, C_BADA + l * 48 + j * 8:C_BADA + l * 48 + (j + 1) * 8].unsqueeze(2).to_broadcast([128, 8, 3]),
                   ALU.add, [("ps", MB), "smallp"], [("mod", l, j)])
                for (gm, jj, cg) in ((gm1, 1, C_GMIX), (gm2, 4, C_GFFN)):
                    if j == jj:
                        ts(gm[:, l, :, :], mod[:, l, j * 8:(j + 1) * 8, :], 1.0, None, ALU.add, None, [("mod", l, j)], [("gm", l, j * 8)])
                        tt(gm[:, l, :, :], gm[:, l, :, :],
                           smallp[:, cg + l * 8:cg + (l + 1) * 8].unsqueeze(2).to_broadcast([128, 8, 3]), ALU.mult,
                           [("gm", l, j * 8), "smallp"], [("gm", l, j * 8)])

            mod_jobs = [(l_, j_, hb_) for l_ in range(DEPTH) for j_ in range(6) for hb_ in range(2)]
            nbk_box[0] = lambda: (NBANK if mod_jobs else 8)

            def mod_step():
                if mod_jobs:
                    l_, j_, hb_ = mod_jobs.pop(0)
                    mod_job(l_, j_, hb_)
                    if hb_ == 1:
                        mod_finish_kind(l_, j_)

            for _ in range(4):
                mod_step()
            for l in range(DEPTH):
                bk = rot("bank", nbk())
                mm([(ps[bk][:, hh * 128:(hh + 1) * 128], ones[:, :], WT[:, l * 4 + hh, :], True, True) for hh in range(4)],
                   ["ones", "WT"], [("ps", bk)])
                for hh in range(4):
                    stt(Bt[:, l * 4 + hh, :], ps[bk][:, hh * 128:(hh + 1) * 128], col(C_BV + l * 4 + hh),
                        tf[l][:, hh * 128:(hh + 1) * 128], ALU.mult, ALU.add,
                        [("ps", bk), "smallp", ("tf", l)], [("Bt", l)])


            tiles = []
            for b in range(2):
                for t0 in range(0, SEQ, TILE):
                    tiles.append((b, t0, TILE, b * SEQ + t0))
            subs_all = []
            tile_subs = []
            for ti, (b, t0, nt, g0) in enumerate(tiles):
                lst = []
                for si, s0 in enumerate(range(0, nt, SUB)):
                    q = len(subs_all)
                    subs_all.append((ti, s0, SUB, g0 + s0))
                    lst.append((si, s0, SUB, q % 3, q, b, False, t0 + s0 == 0, t0 + s0 + SUB == SEQ, g0 + s0))
                tile_subs.append(lst)
            tile_subs[-1].append((2, TILE, DEC, 3, None, 2, True, False, True, 2 * SEQ))
            store_ops = []

            def xload(q):
                if q >= len(subs_all):
                    return
                ti, s0, sn, gg = subs_all[q]
                j = q % 3
                S.op("sp", lambda e: e.dma_start(out=xs[j][:, :, 0:sn], in_=x_fm[:, :, gg:gg + sn]),
                     writes=[("x", j, k) for k in range(KD)], sem=("xld", j))

            for q in range(3):
                xload(q)
            S.op("sp", lambda e: e.dma_start(out=xs[3][:, :, :], in_=x_fm[:, :, 2 * SEQ:2 * SEQ + DEC]),
                 writes=[("x", 3, k) for k in range(KD)], sem=("xld", 3))

            def sqbuf(sub):
                return (sqc, "sqc") if sub[6] else (sq, "sq")

            def n_sq(sub):
                si, s0, sn, xj, q = sub[:5]
                sqb, sqk = sqbuf(sub)
                for k in range(KD):
                    act(sqb[k][:, :sn], xs[xj][:, k, :sn], AF.Square, [("x", xj, k)], [(sqk, k)])

            def n_stats_pe(sub):
                si, s0, sn, xj, q = sub[:5]
                sqb, sqk = sqbuf(sub)
                bk = rot("bank", nbk())
                for k in range(KD):
                    mm([(ps[bk][:, :sn], ones[:, :], sqb[k][:, :sn], k == 0, k == KD - 1)], [(sqk, k), "ones"], [("ps", bk)])
                return bk

            def fin_stats_def(sub):
                si, s0, sn, xj, q = sub[:5]
                bk = n_stats_pe(sub)
                act(rs_fin[:, :sn], ps[bk][:, :sn], AF.Sqrt, [("ps", bk)], ["rsfin"], bias=EPS, scale=1.0 / D)

            def fin_chain_def(sub):
                si, s0, sn, xj, q = sub[:5]
                gg = sub[9]
                x = xs[xj]
                OP("dve", lambda e: e.reciprocal(out=rs_fin[:, :sn], in_=rs_fin[:, :sn]), ["rsfin"], ["rsfin"])
                for k in range(KD):
                    stt(x[:, k, :sn], x[:, k, :sn], col(C_GFIN + k), rs_fin[:, :sn], ALU.mult, ALU.mult,
                        [("x", xj, k), "rsfin", "smallp"], [("x", xj, k)])
                store_ops.append(S.op("sp", lambda e: e.dma_start(out=y_fm[:, :, gg:gg + sn], in_=xs[xj][:, :, 0:sn]),
                                      reads=[("x", xj, k) for k in range(KD)], sem=("xst", xj)))
                if q is not None:
                    xload(q + 3)

            def n_stats(sub):
                si, s0, sn, xj, q = sub[:5]
                bk = n_stats_pe(sub)
                jr = rot("rs", 2)
                act(rs[jr][:, :sn], ps[bk][:, :sn], AF.Sqrt, [("ps", bk)], [("rs", jr)], bias=EPS, scale=1.0 / D)
                OP("dve", lambda e: e.reciprocal(out=rs[jr][:, :sn], in_=rs[jr][:, :sn]), [("rs", jr)], [("rs", jr)])
                return jr

            def n_chain(sub, jr, l, which, b):
                si, s0, sn, xj, q = sub[:5]
                b = sub[5]
                x = xs[xj]
                for k in range(KD):
                    if which == 0:
                        stt(x[:, k, :sn], x[:, k, :sn], col(C_GFIN + k), rs[jr][:, :sn], ALU.mult, ALU.mult,
                            [("x", xj, k), ("rs", jr), "smallp"], [("x", xj, k)])
                    else:
                        gm = gm1 if which == 1 else gm2
                        j0 = 8 if which == 1 else 32
                        shj = 0 if which == 1 else 24
                        jt = rot("tf", NTF)
                        tt(tf[jt][:, :sn], x[:, k, :sn], rs[jr][:, :sn], ALU.mult, [("x", xj, k), ("rs", jr)], [("tf", jt)],
                           eng="dve")
                        act(h[:, k, s0:s0 + sn], tf[jt][:, :sn], AF.Identity, [("tf", jt), ("gm", l, j0), ("mod", l, 0 if which == 1 else 3)],
                            [("h", k, si)], bias=mod[:, l, shj + k, b:b + 1], scale=gm[:, l, k, b:b + 1])

            def n_rest(sub, l, which, b):
                jr = n_stats(sub)
                n_chain(sub, jr, l, which, b)

            def fin_rest(sub, gg):
                si, s0, sn, xj, q = sub[:5]
                gg = sub[9]
                n_rest(sub, 0, 0, 0)
                store_ops.append(S.op("sp", lambda e: e.dma_start(out=y_fm[:, :, gg:gg + sn], in_=xs[xj][:, :, 0:sn]),
                                      reads=[("x", xj, k) for k in range(KD)], sem=("xst", xj)))
                if q is not None:
                    xload(q + 3)

            def a_groups(l, b, t0, is_sample, nsub, sub, slot, blk):
                si, s0, sn, xj, q = sub[:5]
                b, is_sample, seq_start, seq_end = sub[5:9]
                cn = min(128, sn)
                nch = sn // cn
                for c in range(nch):
                    bk = rot("bank", nbk())
                    mm([(ps[bk][0:cn, :], h[:, k, s0 + c * cn:s0 + (c + 1) * cn], blk[:, k, :], k == 0, k == KD - 1)
                        for k in range(KD)], [("ring", slot)] + [("h", k, si) for k in range(KD)], [("ps", bk)])
                    act(atm[c][0:cn, :], ps[bk][0:cn, :], AF.Identity, [("ps", bk)], [("atm", c)])
                    if seq_end and c == nch - 1:
                        act(af32[0:cn, :], ps[bk][0:cn, :], AF.Identity, [("ps", bk)], ["af32"])
                        store_ops.append(S.op("sp", lambda e: e.dma_start(out=pool_out[b, l, :, :], in_=af32[cn - 15:cn, :]),
                                              reads=["af32"], sem=("pst", b, l)))

            def pooling(l, t0, is_sample, sub):
                si, s0, sn, xj, q = sub[:5]
                b, is_sample, seq_start, seq_end = sub[5:9]
                par = si % 2
                dd = d_t2 if par else d_t
                cn = min(128, sn)
                nch = sn // cn
                DM = Dw[:, 0:512].rearrange("p (g t) -> p g t", g=4)
                DH = Dw[:, 512:1024].rearrange("p (g t) -> p g t", g=4)
                DM0 = Dw[:, 1024:1536].rearrange("p (g t) -> p g t", g=4)
                DHS = Dw[:, 1536:1792].rearrange("p (g t) -> p g t", g=4)
                for g in range(4):
                    gc = slice(g * 128, (g + 1) * 128)
                    bk = rot("bank", nbk())
                    mms = []
                    rd = ["Dw"] + [("atm", c) for c in range(nch)]
                    for c in range(nch):
                        o = ps[bk][:, c * cn:(c + 1) * cn]
                        if is_sample:
                            mms.append((o, atm[c][0:cn, gc], DM[0:cn, g, 0:cn], True, False))
                            mms.append((o, statebf[0:15, l, gc], DHS[0:15, g, :], False, True))
                            rd.append("statebf")
                        elif c == 0 and seq_start:
                            mms.append((o, atm[c][:, gc], DM0[:, g, :], True, True))
                        else:
                            mms.append((o, atm[c][:, gc], DM[:, g, :], True, False))
                            if c == 0:
                                mms.append((o, akeep[:, l, gc], DH[:, g, :], False, True))
                                rd.append(("akeep", l))
                            else:
                                mms.append((o, atm[c - 1][:, gc], DH[:, g, :], False, True))
                    mm(mms, rd, [("ps", bk)])
                    act(dd[:, g, :sn], ps[bk][:, :sn], AF.Identity, [("ps", bk)], [("d", par, g)], z=True)
                if not is_sample and not seq_end:
                    cp(akeep[:, l, :], atm[nch - 1][:, :], [("atm", nch - 1)], [("akeep", l)])

            def ya_groups(l, sub):
                si, s0, sn, xj, q = sub[:5]
                par = si % 2
                dd = d_t2 if par else d_t
                for g in range(4):
                    bk = rot("bank", nbk())
                    mm([(ps[bk][:, :sn], wpool[:, l * 4 + g, :], dd[:, g, :sn], True, True)], [("d", par, g), "wpool"],
                       [("ps", bk)], z=True)
                    act(yab[:, g, s0:s0 + sn], ps[bk][:, :sn], AF.Identity, [("ps", bk), "smallp"], [("yab", g, si)],
                        scale=col(C_PSC + l * 4 + g), z=True)

            def u_groups(sub, slot, blk):
                si, s0, sn, xj, q = sub[:5]
                for m in range(4):
                    bk = rot("bank", nbk())
                    mm([(ps[bk][:, :sn], blk[:, k, m * 128:(m + 1) * 128], h[:, k, s0:s0 + sn], k == 0, k == KD - 1)
                        for k in range(KD)], [("ring", slot)] + [("h", k, si) for k in range(KD)], [("ps", bk)])
                    act(u_t[:, m, s0:s0 + sn], ps[bk][:, :sn], AF.Gelu, [("ps", bk)], [("u", m, si)], z=True)

            def v_chunks(sub, slot, blk):
                si, s0, sn, xj, q = sub[:5]
                cn = min(128, sn)
                nch = sn // cn
                vgj = []
                for c in range(nch):
                    bk = rot("bank", nbk())
                    mm([(ps[bk][0:cn, :], h[:, k, s0 + c * cn:s0 + (c + 1) * cn], blk[:, k, :], k == 0, k == KD - 1)
                        for k in range(KD)], [("ring", slot)] + [("h", k, si) for k in range(KD)], [("ps", bk)])
                    j = rot("vg", 4)
                    vgj.append(j)
                    act(vg[j][0:cn, :], ps[bk][0:cn, :], AF.Gelu, [("ps", bk)], [("vg", j)])
                    OP("dve", (lambda j_, c_, cn_: lambda e: e.bn_stats(st6[0:cn_, c_, :], vg[j_][0:cn_, :]))(j, c, cn),
                       [("vg", j)], [("st6", c)])
                    OP("dve", (lambda c_, cn_: lambda e: e.bn_aggr(mv[0:cn_, c_, :], st6[0:cn_, c_, :]))(c, cn),
                       [("st6", c)], [("mv", c)])
                return vgj

            def ln(l, is_sample, sub, vgj):
                si, s0, sn, xj, q = sub[:5]
                is_sample = sub[6]
                cn = min(128, sn)
                nch = sn // cn
                mvk = [("mv", c) for c in range(nch)]
                act(sd[0:cn, 0:nch], mv[0:cn, 0:nch, 1], AF.Sqrt, mvk, ["sd"], bias=EPS)
                OP("dve", lambda e: e.reciprocal(out=sd[0:cn, 0:nch], in_=sd[0:cn, 0:nch]), ["sd"], ["sd"])
                for c in range(nch):
                    j = vgj[c]
                    ts(n_t[0:cn, c, :], vg[j][0:cn, :], mv[0:cn, c, 0:1], sd[0:cn, c:c + 1], ALU.subtract, ALU.mult,
                       [("vg", j), ("mv", c), "sd"], [("n", c)], z=True)
                if is_sample:
                    S.op("sp", lambda e: e.dma_start(out=af32[0:DEC, :], in_=gbrow_d[l:l + 1, 0:512].partition_broadcast(DEC)),
                         writes=["af32"], sem="gbld")
                    S.op("sp", lambda e: e.dma_start(out=rs_fin[0:DEC, :], in_=gbrow_d[l:l + 1, 512:1024].partition_broadcast(DEC)),
                         writes=["rsfin"], sem="gbld2")
                    j = vgj[0]
                    ts(vg[j][0:cn, :], vg[j][0:cn, :], mv[0:cn, 0, 0:1], sd[0:cn, 0:1], ALU.subtract, ALU.mult,
                       [("vg", j), ("mv", 0), "sd"], [("vg", j)])
                    tt(vg[j][0:cn, :], vg[j][0:cn, :], af32[0:cn, :], ALU.mult, [("vg", j), "af32"], [("vg", j)])
                    tt(vg[j][0:cn, :], vg[j][0:cn, :], rs_fin[0:cn, :], ALU.add, [("vg", j), "rsfin"], [("vg", j)])
                    store_ops.append(S.op("sp", lambda e: e.dma_start(out=vout[l, :, :], in_=vg[j][0:DEC, :]),
                                          reads=[("vg", j)], sem=("vst", l)))

            def spatial(l, sub):
                si, s0, sn, xj, q = sub[:5]
                cn = min(128, sn)
                nch = sn // cn
                for hh in range(4):
                    bk = rot("bank", nbk())
                    mm([(ps[bk][:, c * cn:(c + 1) * cn], n_t[0:cn, c, hh * 128:(hh + 1) * 128], WT[0:cn, l * 4 + hh, 0:cn], True, True)
                        for c in range(nch)], [("n", c) for c in range(nch)] + ["WT"], [("ps", bk)], z=True)
                    jt = rot("tf", NTF)
                    stt(tf[jt][:, :sn].rearrange("p (c t) -> p c t", c=nch),
                        ps[bk][:, :sn].rearrange("p (c t) -> p c t", c=nch), col(C_GV + l * 4 + hh),
                        Bt[:, l * 4 + hh, 0:cn].unsqueeze(1).to_broadcast([128, nch, cn]), ALU.mult, ALU.add,
                        [("ps", bk), "smallp", ("Bt", l)], [("tf", jt)])
                    tt(yab[:, 4 + hh, s0:s0 + sn], tf[jt][:, :sn], u_t[:, hh, s0:s0 + sn], ALU.mult,
                       [("tf", jt), ("u", hh, si)], [("yab", 4 + hh, si)], z=True)

            def o_group(l, b, sub, m, slot, blk):
                si, s0, sn, xj, q = sub[:5]
                b = sub[5]
                mq = m % 4
                bk = rot("bank", nbk())
                mm([(ps[bk][:, :sn], blk[:, k, mq * 128:(mq + 1) * 128], yab[:, k, s0:s0 + sn], k == 0, k == KD - 1)
                    for k in range(KD)], [("ring", slot)] + [("yab", k, si) for k in range(KD)], [("ps", bk)], z=True)
                stt(xs[xj][:, m, :sn], ps[bk][:, :sn], mod[:, l, 16 + m, b:b + 1], xs[xj][:, m, :sn],
                    ALU.mult, ALU.add, [("ps", bk), ("mod", l, 2), ("x", xj, m)], [("x", xj, m)])

            def ff1_group(sub, fc, mq, slot, blk):
                si, s0, sn, xj, q = sub[:5]
                bk = rot("bank", nbk())
                mm([(ps[bk][:, :sn], blk[:, k, mq * 128:(mq + 1) * 128], h[:, k, s0:s0 + sn], k == 0, k == KD - 1)
                    for k in range(KD)], [("ring", slot)] + [("h", k, si) for k in range(KD)], [("ps", bk)])
                jt = rot("tf", NTF)
                act(tf[jt][:, :sn], ps[bk][:, :sn], AF.Relu, [("ps", bk)], [("tf", jt)])
                tt(hid[:, fc, s0:s0 + sn], tf[jt][:, :sn], tf[jt][:, :sn], ALU.mult, [("tf", jt)], [("hid", fc, si)], z=True)

            def ff2_group(l, b, sub, m, mq, slot, blk):
                si, s0, sn, xj, q = sub[:5]
                b = sub[5]
                bk = rot("bank", nbk())
                mm([(ps[bk][:, :sn], blk[:, fc, mq * 128:(mq + 1) * 128], hid[:, fc, s0:s0 + sn], fc == 0, fc == 15)
                    for fc in range(16)], [("ring", slot)] + [("hid", fc, si) for fc in range(16)], [("ps", bk)], z=True)
                stt(xs[xj][:, m, :sn], ps[bk][:, :sn], mod[:, l, 40 + m, b:b + 1], xs[xj][:, m, :sn],
                    ALU.mult, ALU.add, [("ps", bk), ("mod", l, 5), ("x", xj, m)], [("x", xj, m)])

            TL = [(ti, l) for ti in range(len(tiles)) for l in range(DEPTH)]
            first = tile_subs[0][0]
            n_sq(first)
            n_rest(first, 0, 1, tiles[0][0])
            pending_fin = None
            for i, (ti, l) in enumerate(TL):
                b, t0, nt, g0 = tiles[ti]
                is_sample = (b == 2)
                subs = tile_subs[ti]
                nsub = len(subs)
                sA = subs[0]
                sB = subs[1] if nsub > 1 else None
                sC = subs[2] if nsub > 2 else None

                def normC(which):
                    if sC is not None:
                        n_rest(sC, l, which, b)
                new_tile = (l == 0)
                zphase()
                if sB is not None and not (new_tile and pending_fin is not None):
                    n_sq(sB)
                if sC is not None:
                    n_sq(sC)
                bA = blk_in(l, 0)
                bU = blk_in(l, 1)
                bV = blk_in(l, 2)
                a_groups(l, b, t0, is_sample, nsub, sA, *bA)
                fin_late = None
                if new_tile and pending_fin is not None:
                    fin_stats_def(pending_fin[0])
                    fin_late = pending_fin[0]
                    pending_fin = None
                    if sB is not None:
                        n_sq(sB)
                    pooling(l, t0, is_sample, sA)
                    if sB is not None:
                        n_rest(sB, l, 1, b)
                    normC(1)
                    u_groups(sA, *bU)
                else:
                    pooling(l, t0, is_sample, sA)
                    if sB is not None:
                        n_rest(sB, l, 1, b)
                    normC(1)
                    u_groups(sA, *bU)
                vgA = v_chunks(sA, *bV)
                ln(l, is_sample, sA, vgA)
                if sB is not None:
                    a_groups(l, b, t0, is_sample, nsub, sB, *bA)
                    if sC is None:
                        release(bA)
                    spatial(l, sA)
                    if fin_late is not None:
                        fin_chain_def(fin_late)
                    pooling(l, t0, is_sample, sB)
                    u_groups(sB, *bU)
                    ya_groups(l, sA)
                    vgB = v_chunks(sB, *bV)
                    ln(l, is_sample, sB, vgB)
                    if sC is not None:
                        a_groups(l, b, t0, is_sample, nsub, sC, *bA)
                        release(bA)
                        u_groups(sC, *bU)
                        vgC = v_chunks(sC, *bV)
                    release(bU)
                    release(bV)
                else:
                    release(bA)
                    release(bU)
                    release(bV)
                    ya_groups(l, sA)
                    spatial(l, sA)
                if ti == 0 and l == 0:
                    for _ in range(6):
                        mod_step()
                bO = [blk_out(l, 0), blk_out(l, 1)]
                for m in range(4):
                    o_group(l, b, sA, m, *bO[0])
                if sB is not None:
                    spatial(l, sB)
                if sC is not None:
                    pooling(l, t0, is_sample, sC)
                    ln(l, is_sample, sC, vgC)
                for m in range(4, 8):
                    o_group(l, b, sA, m, *bO[1])
                    if m == 5 and sB is not None:
                        ya_groups(l, sB)
                if sC is not None:
                    ya_groups(l, sC)
                    spatial(l, sC)
                n_sq(sA)
                if sB is not None:
                    for m in range(8):
                        o_group(l, b, sB, m, *bO[m // 4])
                        if m == 1:
                            n_rest(sA, l, 2, b)
                    if sC is not None:
                        for m in range(8):
                            o_group(l, b, sC, m, *bO[m // 4])
                    release(bO[0])
                    release(bO[1])
                    n_sq(sB)
                    if sC is not None:
                        n_sq(sC)
                else:
                    release(bO[0])
                    release(bO[1])
                    n_rest(sA, l, 2, b)
                for half in range(2):
                    zphase()
                    f1 = [blk_ff1(l, half, 0), blk_ff1(l, half, 1)]
                    if half == 0:
                        for fc in range(8):
                            ff1_group(sA, fc, fc % 4, *f1[fc // 4])
                            if fc == 1 and sB is not None:
                                n_rest(sB, l, 2, b)
                                normC(2)
                        if sB is not None:
                            for fc in range(8):
                                ff1_group(sB, fc, fc % 4, *f1[fc // 4])
                        if sC is not None:
                            for fc in range(8):
                                ff1_group(sC, fc, fc % 4, *f1[fc // 4])
                        release(f1[0])
                        release(f1[1])
                        if ti == 0 and l == 0:
                            mod_step()
                            mod_step()
                    else:
                        for fb in range(2):
                            for sub in subs:
                                for mq in range(4):
                                    ff1_group(sub, fb * 4 + mq, mq, *f1[fb])
                            release(f1[fb])
                        if ti == 0 and l == 0:
                            mod_step()
                            mod_step()
                    for fb in range(2, 4):
                        bf = blk_ff1(l, half, fb)
                        for sub in subs:
                            for mq in range(4):
                                ff1_group(sub, fb * 4 + mq, mq, *bf)
                        release(bf)
                        if ti == 0 and l == 0:
                            mod_step()
                    last_half = (half == 1)
                    nblk_major = 4 if not last_half else 1
                    for mb in range(nblk_major):
                        bf = blk_ff2(l, half, mb)
                        for sub in subs:
                            for mq in range(2):
                                ff2_group(l, b, sub, mb * 2 + mq, mq, *bf)
                        release(bf)
                        if ti == 0 and l == 0:
                            mod_step()
                    if last_half:
                        if ti == 0 and l == 0:
                            while mod_jobs:
                                mod_step()
                        f2 = [blk_ff2(l, half, 1), blk_ff2(l, half, 2), blk_ff2(l, half, 3)]
                        for m in range(2, 8):
                            ff2_group(l, b, sA, m, m % 2, *f2[(m - 2) // 2])
                        last_layer = (l == DEPTH - 1)
                        nxt = TL[i + 1] if i + 1 < len(TL) else None

                        def prologue():
                            if last_layer:
                                fin_rest(sA, sA[9])
                                if nxt is not None:
                                    nsub0 = tile_subs[nxt[0]][0]
                                    n_sq(nsub0)
                                    n_rest(nsub0, 0, 1, tiles[nxt[0]][0])
                            else:
                                n_rest(sA, l + 1, 1, b)

                        n_sq(sA)
                        if sB is not None:
                            for m in range(2, 8):
                                ff2_group(l, b, sB, m, m % 2, *f2[(m - 2) // 2])
                                if m % 2 == 1 and sC is None:
                                    release(f2[(m - 2) // 2])
                                if m == 3:
                                    prologue()
                            if sC is not None:
                                for m in range(2, 8):
                                    ff2_group(l, b, sC, m, m % 2, *f2[(m - 2) // 2])
                                for fb_ in f2:
                                    release(fb_)
                            if last_layer:
                                n_sq(sB)
                                if nxt is not None:
                                    pending_fin = (sB, sB[9])
                                else:
                                    fin_rest(sB, sB[9])
                                    if sC is not None:
                                        n_sq(sC)
                                        fin_rest(sC, sC[9])
                        else:
                            for fb_ in f2:
                                release(fb_)
                            prologue()
            S.op("sp", None, after=store_ops)
            return collected

        plan = record(Sched(), None)
        S = Sched()
        record(S, plan)
        S.emit(nc)
    return nc


def _fm(v, nchunk):
    return np.ascontiguousarray(v.reshape(nchunk, 128).T)


def kernel(x_prompt, x_sample, state_pool, c_prompt, c_sample, w_ada, b_ada, norm_mix_g, w_in,
           w_pool, pool_scale, v_norm_g, v_norm_b, w_spatial, b_spatial, w_out, norm_ffn_g,
           w_ff1, w_ff2, final_norm_g):
    f32 = np.float32
    A = lambda a: np.ascontiguousarray(np.asarray(a, dtype=f32))
    x_prompt, x_sample, state_pool = A(x_prompt), A(x_sample), A(state_pool)
    c_prompt, c_sample = A(c_prompt), A(c_sample)
    w_ada, w_in, w_out, w_ff1, w_ff2 = A(w_ada), A(w_in), A(w_out), A(w_ff1), A(w_ff2)
    b_ada, norm_mix_g, norm_ffn_g, final_norm_g = A(b_ada), A(norm_mix_g), A(norm_ffn_g), A(final_norm_g)
    w_pool, pool_scale, v_norm_g, v_norm_b = A(w_pool), A(pool_scale), A(v_norm_g), A(v_norm_b)
    w_spatial, b_spatial = A(w_spatial), A(b_spatial)

    shared_small = np.zeros((128, NSMALL), f32)
    for l in range(DEPTH):
        shared_small[:, C_BADA + l * 48:C_BADA + (l + 1) * 48] = _fm(b_ada[l], 48)
        shared_small[:, C_GMIX + l * 8:C_GMIX + (l + 1) * 8] = _fm(norm_mix_g[l], 8)
        shared_small[:, C_GFFN + l * 8:C_GFFN + (l + 1) * 8] = _fm(norm_ffn_g[l], 8)
        shared_small[:, C_PSC + l * 4:C_PSC + (l + 1) * 4] = _fm(pool_scale[l], 4)
        shared_small[:, C_GV + l * 4:C_GV + (l + 1) * 4] = _fm(v_norm_g[l], 4)
        shared_small[:, C_BV + l * 4:C_BV + (l + 1) * 4] = _fm(v_norm_b[l], 4)
    shared_small[:, C_GFIN:C_GFIN + 8] = _fm(final_norm_g, 8)
    for g, w in enumerate(POOL_W):
        for p in range(16):
            shared_small[:, C_INV + g * 16 + p] = 1.0 / min(p + 1, w)
    bsrow = np.ascontiguousarray(b_spatial.reshape(DEPTH, 512))
    gbrow = np.ascontiguousarray(np.concatenate([v_norm_g, v_norm_b], axis=1))
    maskt = np.ascontiguousarray(np.triu(np.ones((128, 128), f32)))
    wspT = np.ascontiguousarray(w_spatial.transpose(3, 0, 1, 2).reshape(128, DEPTH * 4 * 128))
    wpool_l = np.ascontiguousarray(w_pool.transpose(2, 0, 1, 3).reshape(128, DEPTH * 4 * 128))

    tt_ = np.arange(128)[None, :]
    ss_ = np.arange(128)[:, None]
    dwin = np.zeros((128, 3 * 4 * 128 + 4 * 64), f32)
    for g, w in enumerate(POOL_W):
        eye = (ss_ == tt_).astype(f32)
        dm = ((ss_ <= tt_) & (ss_ > tt_ - w)).astype(f32) / w - eye
        dh = (ss_ > tt_ - w + 128).astype(f32) / w
        cnt = np.minimum(tt_ + 1, w).astype(f32)
        dm0 = ((ss_ <= tt_) & (ss_ > tt_ - w)).astype(f32) / cnt - eye
        dhs = (ss_[:, :] > tt_[:, :64] - w + 15).astype(f32) / w
        dhs[15:, :] = 0.0
        dwin[:, 0 + g * 128:0 + (g + 1) * 128] = dm
        dwin[:, 512 + g * 128:512 + (g + 1) * 128] = dh
        dwin[:, 1024 + g * 128:1024 + (g + 1) * 128] = dm0
        dwin[:, 1536 + g * 64:1536 + (g + 1) * 64] = dhs

    in_maps = []
    for i in range(NCORES):
        xt = np.concatenate([x_prompt[2 * i], x_prompt[2 * i + 1], x_sample[i]], axis=0)
        x_fm = np.ascontiguousarray(xt.reshape(NTOK, KD, 128).transpose(2, 1, 0))
        sp = shared_small.copy()
        cs = np.stack([c_prompt[2 * i], c_prompt[2 * i + 1], c_sample[i]], axis=0)
        sp[:, C_C:C_C + 24] = cs.reshape(3, KD, 128).transpose(2, 1, 0).reshape(128, 24)
        in_maps.append({
            "x_fm": x_fm, "smallp": sp, "bsrow": bsrow, "gbrow": gbrow, "maskt": maskt, "wspT": wspT,
            "wpool_l": wpool_l, "statein": np.ascontiguousarray(state_pool[:, i]), "dwin": dwin,
            "w_ada": w_ada, "w_in": w_in, "w_out": w_out, "w_ff1": w_ff1, "w_ff2": w_ff2,
        })
    nc = build_nc()
    res = run_bass_kernel_spmd(nc, in_maps, core_ids=list(range(NCORES)))

    y_prompt = np.empty((16, SEQ, D), f32)
    y_sample = np.empty((8, DEC, D), f32)
    sp_prompt = np.empty((DEPTH, 16, 15, 512), f32)
    sp_sample = np.empty((DEPTH, 8, 15, 512), f32)
    sv = np.empty((DEPTH, 8, DEC, 512), f32)
    for i in range(NCORES):
        r = res.results[i]
        yt = np.asarray(r["y_fm"]).transpose(2, 1, 0).reshape(NTOK, D)
        y_prompt[2 * i] = yt[0:SEQ]
        y_prompt[2 * i + 1] = yt[SEQ:2 * SEQ]
        y_sample[i] = yt[2 * SEQ:]
        po = np.asarray(r["pool_out"])
        sp_prompt[:, 2 * i] = po[0]
        sp_prompt[:, 2 * i + 1] = po[1]
        sp_sample[:, i] = po[2]
        sv[:, i] = np.asarray(r["vout"])
    return (y_prompt, y_sample, sp_prompt, sp_sample, sv)
```

```python
import os
import sys
import numpy as np
from contextlib import ExitStack
import concourse.bass as bass
import concourse.mybir as mybir
from concourse.bass_utils import run_bass_kernel_spmd

F32 = mybir.dt.float32
BF16 = mybir.dt.bfloat16
AF = mybir.ActivationFunctionType
ALU = mybir.AluOpType

NCORES = 8
D = 1024
KD = 8
DEPTH = 2
SEQ = 4096
DEC = 64
NTOK = 2 * SEQ + DEC
TILE = 1024
TILEW = TILE + 64
SUB = 512
EPS = 1e-6
RING = 5
NTF = 5
NBANK = 7
POOL_W = (2, 4, 8, 16)

C_C = 0
C_BADA = 24
C_GMIX = 120
C_GFFN = 136
C_GFIN = 152
C_PSC = 160
C_GV = 168
C_BV = 176
C_INV = 184
NSMALL = 248


_KDEBUG = bool(os.environ.get("KDEBUG"))
_KTAGS = {}


class Op:
    __slots__ = ("eng", "fn", "deps", "signal", "sem", "token", "idx", "is_dma", "tag")

    def __init__(self, eng, fn, idx, sem=None):
        self.eng = eng
        self.fn = fn
        self.deps = []
        self.signal = False
        self.sem = sem
        self.token = None
        self.idx = idx
        self.is_dma = sem is not None


class Sched:
    ENGS = ("pe", "act", "dve", "pool", "sp")

    def __init__(self):
        self.ops = []
        self.last_w = {}
        self.readers = {}
        self.total_wait = set()

    def op(self, eng, fn, reads=(), writes=(), after=(), sem=None):
        o = Op(eng, fn, len(self.ops), sem)
        o.tag = None
        if _KDEBUG:
            f = sys._getframe(1)
            names = []
            while f is not None and len(names) < 6:
                nm = f.f_code.co_name
                if nm not in ("OP", "act", "tt", "ts", "stt", "cp", "mm", "<lambda>"):
                    names.append("%s:%d" % (nm, f.f_lineno))
                f = f.f_back
            o.tag = "|".join(names[:3])
        deps = {}
        for k in reads:
            w = self.last_w.get(k)
            if w is not None:
                deps[w.idx] = w
        for k in writes:
            w = self.last_w.get(k)
            if w is not None:
                deps[w.idx] = w
            for r in self.readers.get(k, {}).values():
                if isinstance(r, list):
                    for rr in r:
                        deps[rr.idx] = rr
                else:
                    deps[r.idx] = r
        for a in after:
            if a is not None:
                deps[a.idx] = a
        for k in reads:
            d = self.readers.setdefault(k, {})
            if o.is_dma:
                d.setdefault("dma", []).append(o)
            else:
                d[eng] = o
        for k in writes:
            self.last_w[k] = o
            self.readers[k] = {}
        o.deps = list(deps.values())
        for d in o.deps:
            d.signal = True
        self.ops.append(o)
        return o

    def emit(self, nc):
        semkeys = list(self.ENGS[:4])
        for o in self.ops:
            if o.is_dma and o.sem not in semkeys:
                semkeys.append(o.sem)
        counters = {k: 0 for k in semkeys}
        for o in self.ops:
            if o.is_dma:
                counters[o.sem] += 16
                o.token = (o.sem, counters[o.sem])
            elif o.signal:
                counters[o.eng] += 1
                o.token = (o.eng, counters[o.eng])
        for o in self.ops:
            if o.is_dma and o.sem in self.total_wait:
                o.token = (o.sem, counters[o.sem])
        per_eng = {e: [o for o in self.ops if o.eng == e] for e in self.ENGS}
        with ExitStack() as es:
            sems = {k: es.enter_context(nc.semaphore("s_%d" % i)) for i, k in enumerate(semkeys)}
            block = es.enter_context(nc.Block())

            def run(eng_name):
                def body(h):
                    waited = {}
                    for o in per_eng[eng_name]:
                        for d in o.deps:
                            if (not d.is_dma) and d.eng == "pe" and eng_name == "pe":
                                continue
                            sk, val = d.token
                            if waited.get(sk, 0) >= val:
                                continue
                            waited[sk] = val
                            h.wait_ge(sems[sk], val)
                        ins = o.fn(h) if o.fn is not None else None
                        if _KDEBUG and ins is not None:
                            _KTAGS[str(ins.ins.name)] = o.tag
                        if o.is_dma:
                            ins.then_inc(sems[o.sem], 16)
                        elif o.signal:
                            ins.then_inc(sems[o.eng], 1)
                return body

            block.tensor(run("pe"))
            block.scalar(run("act"))
            block.vector(run("dve"))
            block.gpsimd(run("pool"))
            block.sync(run("sp"))


def build_nc():
    nc = bass.Bass("TRN2", target_bir_lowering=False)
    dt_in = lambda n, s: nc.dram_tensor(n, s, F32, kind="ExternalInput").ap()
    dt_out = lambda n, s: nc.dram_tensor(n, s, F32, kind="ExternalOutput").ap()
    x_fm = dt_in("x_fm", [128, KD, NTOK])
    smallp_d = dt_in("smallp", [128, NSMALL])
    bsrow_d = dt_in("bsrow", [DEPTH, 512])
    gbrow_d = dt_in("gbrow", [DEPTH, 1024])
    maskt_d = dt_in("maskt", [128, 128])
    wspT_d = dt_in("wspT", [128, DEPTH * 4 * 128])
    wpool_d = dt_in("wpool_l", [128, DEPTH * 4 * 128])
    statein_d = dt_in("statein", [DEPTH, 15, 512])
    dwin_d = dt_in("dwin", [128, 3 * 4 * 128 + 4 * 64])
    w_ada = dt_in("w_ada", [DEPTH, D, 6 * D])
    w_in = dt_in("w_in", [DEPTH, D, 1536])
    w_out = dt_in("w_out", [DEPTH, D, D])
    w_ff1 = dt_in("w_ff1", [DEPTH, D, 4 * D])
    w_ff2 = dt_in("w_ff2", [DEPTH, 4 * D, D])
    y_fm = dt_out("y_fm", [128, KD, NTOK])
    pool_out = dt_out("pool_out", [3, DEPTH, 15, 512])
    vout = dt_out("vout", [DEPTH, DEC, 512])

    with ExitStack() as es:
        sb = lambda n, s, d: es.enter_context(nc.sbuf_tensor(n, s, d))
        xs = [sb("xs%d" % j, [128, KD, SUB], F32) for j in range(3)] + [sb("xs3", [128, KD, DEC], F32)]
        h = sb("h", [128, KD, TILEW], BF16)
        Z = sb("Z", [128, 16 * TILEW], BF16)
        hid = Z[:, :].rearrange("p (f t) -> p f t", f=16)
        yab = Z[:, 0:8 * TILEW].rearrange("p (k t) -> p k t", k=8)
        u_t = Z[:, 8 * TILEW:12 * TILEW].rearrange("p (k t) -> p k t", k=4)
        n_t = Z[:, 12 * TILEW:12 * TILEW + 2048].rearrange("p (c n) -> p c n", c=4)
        d_t = Z[:, 12 * TILEW + 2048:12 * TILEW + 4096].rearrange("p (g t) -> p g t", g=4)
        sq = [sb("sq%d" % j, [128, SUB], BF16) for j in range(KD)]
        rs = [sb("rs%d" % j, [128, SUB], F32) for j in range(2)]
        rs_fin = sb("rs_fin", [128, SUB], F32)
        sqc = [sb("sqc%d" % j, [128, DEC], BF16) for j in range(KD)]
        tf = [sb("tf%d" % j, [128, SUB], F32) for j in range(NTF)]
        vg = [sb("vg%d" % j, [128, 512], F32) for j in range(4)]
        atm = [sb("atm%d" % j, [128, 512], BF16) for j in range(4)]
        akeep = sb("akeep", [128, DEPTH, 512], BF16)
        statebf = sb("statebf", [128, DEPTH, 512], BF16)
        af32 = sb("af32", [128, 512], F32)
        Dw = sb("Dw", [128, 3 * 4 * 128 + 4 * 64], BF16)
        st6 = sb("st6", [128, 4, 6], F32)
        mv = sb("mv", [128, 4, 2], F32)
        sd = sb("sd", [128, 4], F32)
        d_t2 = sb("d_t2", [128, 4, SUB], BF16)
        ring = [sb("ring%d" % j, [128, 4096], BF16) for j in range(RING)]
        smallp = sb("smallp_sb", [128, NSMALL], F32)
        scb = sb("scb", [128, KD, 3], BF16)
        mod = sb("mod", [128, DEPTH, 48, 3], F32)
        gm1 = sb("gm1", [128, DEPTH, KD, 3], F32)
        gm2 = sb("gm2", [128, DEPTH, KD, 3], F32)
        WTraw = sb("WTraw", [128, DEPTH * 4, 128], BF16)
        WT = sb("WT", [128, DEPTH * 4, 128], BF16)
        maskb = sb("maskb", [128, 128], BF16)
        Bt = sb("Bt", [128, DEPTH * 4, 128], F32)
        wpool = sb("wpool", [128, DEPTH * 4, 128], BF16)
        ones = sb("ones", [128, 128], BF16)
        ps = [es.enter_context(nc.psum_tensor("ps%d" % i, [128, 512], F32)) for i in range(8)]

        def record(S, plan):
            collected = []
            cnt = {"bank": 0, "rs": 0, "tf": 0, "vg": 0}

            def rot(name, n):
                v = cnt[name] % n
                cnt[name] += 1
                return v

            zstate = {"prev": {}, "cur": {}}

            def zphase():
                zstate["prev"] = zstate["cur"]
                zstate["cur"] = {}

            def OP(eng, fn, reads=(), writes=(), z=False, after=(), sem=None):
                aft = list(after)
                if z:
                    aft += list(zstate["prev"].values())
                o = S.op(eng, fn, reads, writes, aft, sem)
                if z:
                    zstate["cur"][eng] = o
                return o

            def act(out, in_, func, reads, writes, bias=0.0, scale=1.0, z=False):
                return OP("act", lambda e: e.activation(out=out, in_=in_, func=func, bias=bias, scale=scale),
                          reads, writes, z)

            def tt(out, in0, in1, op, reads, writes, z=False, eng="dve"):
                return OP(eng, lambda e: e.tensor_tensor(out=out, in0=in0, in1=in1, op=op), reads, writes, z)

            def ts(out, in0, s1, s2, op0, op1, reads, writes, z=False, eng="dve"):
                if op1 is None:
                    return OP(eng, lambda e: e.tensor_scalar(out, in0, s1, None, op0), reads, writes, z)
                return OP(eng, lambda e: e.tensor_scalar(out, in0, s1, s2, op0, op1), reads, writes, z)

            def stt(out, in0, scalar, in1, op0, op1, reads, writes, z=False, eng="dve"):
                return OP(eng, lambda e: e.scalar_tensor_tensor(out=out, in0=in0, scalar=scalar, in1=in1,
                                                                op0=op0, op1=op1), reads, writes, z)

            def cp(out, in_, reads, writes, z=False, eng="dve"):
                return OP(eng, lambda e: e.tensor_copy(out=out, in_=in_), reads, writes, z)

            def mm(mms, reads, writes, z=False):
                def fn(e):
                    ins = None
                    for (o_, l_, r_, st_, sp_) in mms:
                        ins = e.matmul(o_, l_, r_, start=st_, stop=sp_)
                    return ins
                return OP("pe", fn, reads, writes, z)

            def col(c):
                return smallp[:, c:c + 1]

            nbk_box = [lambda: NBANK]

            def nbk():
                return nbk_box[0]()

            bstate = {"n": 0, "next_load": 0}
            slot_idx = {}

            def rec_load(i, src, kk):
                slot = i % RING
                dst = ring[slot][:, :].rearrange("p (k n) -> p k n", k=kk)
                S.op("pool", lambda e: e.dma_start(out=dst, in_=src), writes=[("ring", slot)], sem=("ring", slot))

            def acquire(src, kk):
                i = bstate["n"]
                bstate["n"] += 1
                slot = i % RING
                slot_idx[slot] = i
                if plan is None:
                    collected.append((src, kk))
                    rec_load(i, src, kk)
                else:
                    assert i < bstate["next_load"], "ring block used before its load was recorded"
                return slot, ring[slot][:, :].rearrange("p (k n) -> p k n", k=kk)

            def release(b):
                if plan is None:
                    return
                j = slot_idx[b[0]] + RING
                assert j == bstate["next_load"], "blocks must be released in acquisition order"
                if j < len(plan):
                    rec_load(j, *plan[j])
                bstate["next_load"] = j + 1

            def blk_in(l, i):
                return acquire(w_in[l, :, i * 512:(i + 1) * 512].rearrange("(k p) n -> p k n", p=128), 8)

            def blk_out(l, i):
                return acquire(w_out[l, :, i * 512:(i + 1) * 512].rearrange("(k p) n -> p k n", p=128), 8)

            def blk_ff1(l, half, fb):
                c0 = (half * 4 + fb) * 512
                return acquire(w_ff1[l, :, c0:c0 + 512].rearrange("(k p) n -> p k n", p=128), 8)

            def blk_ff2(l, half, mb):
                return acquire(w_ff2[l, half * 2048:(half + 1) * 2048, mb * 256:(mb + 1) * 256]
                               .rearrange("(f p) n -> p f n", p=128), 16)

            S.total_wait.add("setup")
            SPL = lambda out, in_, w: S.op("sp", lambda e: e.dma_start(out=out, in_=in_), writes=w, sem="setup")
            SPL(smallp[:, :], smallp_d[:, :], ["smallp"])
            if plan is not None:
                for i0 in range(min(2, len(plan))):
                    rec_load(i0, *plan[i0])
            S.total_wait.add("setupc")
            PLL = lambda out, in_, w: S.op("pool", lambda e: e.dma_start(out=out, in_=in_), writes=w, sem="setupc")
            PLL(WTraw[:, :, :], wspT_d.rearrange("p (a t) -> p a t", t=128), ["WTraw"])
            PLL(wpool[:, :, :], wpool_d.rearrange("p (a t) -> p a t", t=128), ["wpool"])
            PLL(maskb[:, :], maskt_d[:, :], ["maskb"])
            PLL(Dw[:, :], dwin_d[:, :], ["Dw"])
            PLL(statebf[0:15, :, :], statein_d.rearrange("l r c -> r l c"), ["statebf"])
            for l in range(DEPTH):
                SPL(tf[l][:, :], bsrow_d[l:l + 1, :].partition_broadcast(128), [("tf", l)])
            if plan is not None:
                for i0 in range(min(2, len(plan)), min(RING, len(plan))):
                    rec_load(i0, *plan[i0])
                bstate["next_load"] = min(RING, len(plan))
            OP("dve", lambda e: e.memset(ones[:, :], 1.0), writes=["ones"])
            tt(WT[:, :, :], WTraw[:, :, :], maskb[:, :].unsqueeze(1).to_broadcast([128, DEPTH * 4, 128]), ALU.mult,
               ["WTraw", "maskb"], ["WT"])
            act(scb[:, :, :], smallp[:, C_C:C_C + 24].rearrange("p (k b) -> p k b", b=3), AF.Silu, ["smallp"], ["scb"])
            MB = 7

            mod_ops = []

            def mod_job(l, j, hb):
                c0 = j * 1024 + hb * 512
                slot, blk = acquire(w_ada[l, :, c0:c0 + 512].rearrange("(k p) n -> p k n", p=128), 8)
                mms = []
                for m in range(4):
                    cc = (j * 8 + hb * 4 + m) * 3
                    for k in range(KD):
                        mms.append((ps[MB][:, cc:cc + 3], blk[:, k, m * 128:(m + 1) * 128], scb[:, k, :],
                                    k == 0, k == KD - 1))
                mod_ops.append(mm(mms, [("ring", slot), "scb"], [("ps", MB)]))
                release((slot, blk))

            def mod_finish_kind(l, j):
                tt(mod[:, l, j * 8:(j + 1) * 8, :], ps[MB][:, j * 24:(j + 1) * 24].rearrange("p (j b) -> p j b", b=3),
                   smallp[:, C_BADA + l * 48 + j * 8:C_BADA + l * 48 + (j + 1) * 8].unsqueeze(2).to_broadcast([128, 8, 3]),
                   ALU.add, [("ps", MB), "smallp"], [("mod", l, j)])
                for (gm, jj, cg) in ((gm1, 1, C_GMIX), (gm2, 4, C_GFFN)):
                    if j == jj:
                        ts(gm[:, l, :, :], mod[:, l, j * 8:(j + 1) * 8, :], 1.0, None, ALU.add, None, [("mod", l, j)], [("gm", l, j * 8)])
                        tt(gm[:, l, :, :], gm[:, l, :, :],
                           smallp[:, cg + l * 8:cg + (l + 1) * 8].unsqueeze(2).to_broadcast([128, 8, 3]), ALU.mult,
                           [("gm", l, j * 8), "smallp"], [("gm", l, j * 8)])

            mod_jobs = [(l_, j_, hb_) for l_ in range(DEPTH) for j_ in range(6) for hb_ in range(2)]
            nbk_box[0] = lambda: (NBANK if mod_jobs else 8)

            def mod_step():
                if mod_jobs:
                    l_, j_, hb_ = mod_jobs.pop(0)
                    mod_job(l_, j_, hb_)
                    if hb_ == 1:
                        mod_finish_kind(l_, j_)

            for _ in range(4):
                mod_step()
            for l in range(DEPTH):
                bk = rot("bank", nbk())
                mm([(ps[bk][:, hh * 128:(hh + 1) * 128], ones[:, :], WT[:, l * 4 + hh, :], True, True) for hh in range(4)],
                   ["ones", "WT"], [("ps", bk)])
                for hh in range(4):
                    stt(Bt[:, l * 4 + hh, :], ps[bk][:, hh * 128:(hh + 1) * 128], col(C_BV + l * 4 + hh),
                        tf[l][:, hh * 128:(hh + 1) * 128], ALU.mult, ALU.add,
                        [("ps", bk), "smallp", ("tf", l)], [("Bt", l)])


            tiles = []
            for b in range(2):
                for t0 in range(0, SEQ, TILE):
                    tiles.append((b, t0, TILE, b * SEQ + t0))
            subs_all = []
            tile_subs = []
            for ti, (b, t0, nt, g0) in enumerate(tiles):
                lst = []
                for si, s0 in enumerate(range(0, nt, SUB)):
                    q = len(subs_all)
                    subs_all.append((ti, s0, SUB, g0 + s0))
                    lst.append((si, s0, SUB, q % 3, q, b, False, t0 + s0 == 0, t0 + s0 + SUB == SEQ, g0 + s0))
                tile_subs.append(lst)
            tile_subs[-1].append((2, TILE, DEC, 3, None, 2, True, False, True, 2 * SEQ))
            store_ops = []

            def xload(q, after=()):
                if q >= len(subs_all):
                    return
                ti, s0, sn, gg = subs_all[q]
                j = q % 3
                S.op("sp", lambda e: e.dma_start(out=xs[j][:, :, 0:sn], in_=x_fm[:, :, gg:gg + sn]),
                     writes=[("x", j, k) for k in range(KD)], after=after, sem=("xld", j))

            xload(0)
            xload(1)
            xload(2, after=mod_ops[-1:])
            S.op("sp", lambda e: e.dma_start(out=xs[3][:, :, :], in_=x_fm[:, :, 2 * SEQ:2 * SEQ + DEC]),
                 writes=[("x", 3, k) for k in range(KD)], after=mod_ops[-1:], sem=("xld", 3))

            def sqbuf(sub):
                return (sqc, "sqc") if sub[6] else (sq, "sq")

            def n_sq(sub):
                si, s0, sn, xj, q = sub[:5]
                sqb, sqk = sqbuf(sub)
                for k in range(KD):
                    act(sqb[k][:, :sn], xs[xj][:, k, :sn], AF.Square, [("x", xj, k)], [(sqk, k)])

            def n_stats_pe(sub):
                si, s0, sn, xj, q = sub[:5]
                sqb, sqk = sqbuf(sub)
                bk = rot("bank", nbk())
                for k in range(KD):
                    mm([(ps[bk][:, :sn], ones[:, :], sqb[k][:, :sn], k == 0, k == KD - 1)], [(sqk, k), "ones"], [("ps", bk)])
                return bk

            def fin_stats_def(sub):
                si, s0, sn, xj, q = sub[:5]
                bk = n_stats_pe(sub)
                act(rs_fin[:, :sn], ps[bk][:, :sn], AF.Sqrt, [("ps", bk)], ["rsfin"], bias=EPS, scale=1.0 / D)

            def fin_chain_def(sub):
                si, s0, sn, xj, q = sub[:5]
                gg = sub[9]
                x = xs[xj]
                OP("dve", lambda e: e.reciprocal(out=rs_fin[:, :sn], in_=rs_fin[:, :sn]), ["rsfin"], ["rsfin"])
                for k in range(KD):
                    stt(x[:, k, :sn], x[:, k, :sn], col(C_GFIN + k), rs_fin[:, :sn], ALU.mult, ALU.mult,
                        [("x", xj, k), "rsfin", "smallp"], [("x", xj, k)])
                store_ops.append(S.op("sp", lambda e: e.dma_start(out=y_fm[:, :, gg:gg + sn], in_=xs[xj][:, :, 0:sn]),
                                      reads=[("x", xj, k) for k in range(KD)], sem=("xst", xj)))
                if q is not None:
                    xload(q + 3)

            def n_stats(sub):
                si, s0, sn, xj, q = sub[:5]
                bk = n_stats_pe(sub)
                jr = rot("rs", 2)
                act(rs[jr][:, :sn], ps[bk][:, :sn], AF.Sqrt, [("ps", bk)], [("rs", jr)], bias=EPS, scale=1.0 / D)
                OP("dve", lambda e: e.reciprocal(out=rs[jr][:, :sn], in_=rs[jr][:, :sn]), [("rs", jr)], [("rs", jr)])
                return jr

            def n_chain(sub, jr, l, which, b):
                si, s0, sn, xj, q = sub[:5]
                b = sub[5]
                x = xs[xj]
                for k in range(KD):
                    if which == 0:
                        stt(x[:, k, :sn], x[:, k, :sn], col(C_GFIN + k), rs[jr][:, :sn], ALU.mult, ALU.mult,
                            [("x", xj, k), ("rs", jr), "smallp"], [("x", xj, k)])
                    else:
                        gm = gm1 if which == 1 else gm2
                        j0 = 8 if which == 1 else 32
                        shj = 0 if which == 1 else 24
                        jt = rot("tf", NTF)
                        tt(tf[jt][:, :sn], x[:, k, :sn], rs[jr][:, :sn], ALU.mult, [("x", xj, k), ("rs", jr)], [("tf", jt)],
                           eng="dve")
                        act(h[:, k, s0:s0 + sn], tf[jt][:, :sn], AF.Identity, [("tf", jt), ("gm", l, j0), ("mod", l, 0 if which == 1 else 3)],
                            [("h", k, si)], bias=mod[:, l, shj + k, b:b + 1], scale=gm[:, l, k, b:b + 1])

            def n_rest(sub, l, which, b):
                jr = n_stats(sub)
                n_chain(sub, jr, l, which, b)

            def fin_rest(sub, gg):
                si, s0, sn, xj, q = sub[:5]
                gg = sub[9]
                n_rest(sub, 0, 0, 0)
                store_ops.append(S.op("sp", lambda e: e.dma_start(out=y_fm[:, :, gg:gg + sn], in_=xs[xj][:, :, 0:sn]),
                                      reads=[("x", xj, k) for k in range(KD)], sem=("xst", xj)))
                if q is not None:
                    xload(q + 3)

            def a_groups(l, b, t0, is_sample, nsub, sub, slot, blk):
                si, s0, sn, xj, q = sub[:5]
                b, is_sample, seq_start, seq_end = sub[5:9]
                cn = min(128, sn)
                nch = sn // cn
                for c in range(nch):
                    bk = rot("bank", nbk())
                    mm([(ps[bk][0:cn, :], h[:, k, s0 + c * cn:s0 + (c + 1) * cn], blk[:, k, :], k == 0, k == KD - 1)
                        for k in range(KD)], [("ring", slot)] + [("h", k, si) for k in range(KD)], [("ps", bk)])
                    act(atm[c][0:cn, :], ps[bk][0:cn, :], AF.Identity, [("ps", bk)], [("atm", c)])
                    if seq_end and c == nch - 1:
                        act(af32[0:cn, :], ps[bk][0:cn, :], AF.Identity, [("ps", bk)], ["af32"])
                        store_ops.append(S.op("sp", lambda e: e.dma_start(out=pool_out[b, l, :, :], in_=af32[cn - 15:cn, :]),
                                              reads=["af32"], sem=("pst", b, l)))

            def pooling(l, t0, is_sample, sub):
                si, s0, sn, xj, q = sub[:5]
                b, is_sample, seq_start, seq_end = sub[5:9]
                par = si % 2
                dd = d_t2 if par else d_t
                cn = min(128, sn)
                nch = sn // cn
                DM = Dw[:, 0:512].rearrange("p (g t) -> p g t", g=4)
                DH = Dw[:, 512:1024].rearrange("p (g t) -> p g t", g=4)
                DM0 = Dw[:, 1024:1536].rearrange("p (g t) -> p g t", g=4)
                DHS = Dw[:, 1536:1792].rearrange("p (g t) -> p g t", g=4)
                for g in range(4):
                    gc = slice(g * 128, (g + 1) * 128)
                    bk = rot("bank", nbk())
                    mms = []
                    rd = ["Dw"] + [("atm", c) for c in range(nch)]
                    for c in range(nch):
                        o = ps[bk][:, c * cn:(c + 1) * cn]
                        if is_sample:
                            mms.append((o, atm[c][0:cn, gc], DM[0:cn, g, 0:cn], True, False))
                            mms.append((o, statebf[0:15, l, gc], DHS[0:15, g, :], False, True))
                            rd.append("statebf")
                        elif c == 0 and seq_start:
                            mms.append((o, atm[c][:, gc], DM0[:, g, :], True, True))
                        else:
                            mms.append((o, atm[c][:, gc], DM[:, g, :], True, False))
                            if c == 0:
                                mms.append((o, akeep[:, l, gc], DH[:, g, :], False, True))
                                rd.append(("akeep", l))
                            else:
                                mms.append((o, atm[c - 1][:, gc], DH[:, g, :], False, True))
                    mm(mms, rd, [("ps", bk)])
                    act(dd[:, g, :sn], ps[bk][:, :sn], AF.Identity, [("ps", bk)], [("d", par, g)], z=True)
                if not is_sample and not seq_end:
                    cp(akeep[:, l, :], atm[nch - 1][:, :], [("atm", nch - 1)], [("akeep", l)])

            def ya_groups(l, sub):
                si, s0, sn, xj, q = sub[:5]
                par = si % 2
                dd = d_t2 if par else d_t
                for g in range(4):
                    bk = rot("bank", nbk())
                    mm([(ps[bk][:, :sn], wpool[:, l * 4 + g, :], dd[:, g, :sn], True, True)], [("d", par, g), "wpool"],
                       [("ps", bk)], z=True)
                    act(yab[:, g, s0:s0 + sn], ps[bk][:, :sn], AF.Identity, [("ps", bk), "smallp"], [("yab", g, si)],
                        scale=col(C_PSC + l * 4 + g), z=True)

            def u_groups(sub, slot, blk, ms=(0, 1, 2, 3)):
                si, s0, sn, xj, q = sub[:5]
                for m in ms:
                    bk = rot("bank", nbk())
                    mm([(ps[bk][:, :sn], blk[:, k, m * 128:(m + 1) * 128], h[:, k, s0:s0 + sn], k == 0, k == KD - 1)
                        for k in range(KD)], [("ring", slot)] + [("h", k, si) for k in range(KD)], [("ps", bk)])
                    act(u_t[:, m, s0:s0 + sn], ps[bk][:, :sn], AF.Gelu, [("ps", bk)], [("u", m, si)], z=True)

            def v_chunks(sub, slot, blk):
                si, s0, sn, xj, q = sub[:5]
                cn = min(128, sn)
                nch = sn // cn
                vgj = []
                for c in range(nch):
                    bk = rot("bank", nbk())
                    mm([(ps[bk][0:cn, :], h[:, k, s0 + c * cn:s0 + (c + 1) * cn], blk[:, k, :], k == 0, k == KD - 1)
                        for k in range(KD)], [("ring", slot)] + [("h", k, si) for k in range(KD)], [("ps", bk)])
                    j = rot("vg", 4)
                    vgj.append(j)
                    act(vg[j][0:cn, :], ps[bk][0:cn, :], AF.Gelu, [("ps", bk)], [("vg", j)])
                    OP("dve", (lambda j_, c_, cn_: lambda e: e.bn_stats(st6[0:cn_, c_, :], vg[j_][0:cn_, :]))(j, c, cn),
                       [("vg", j)], [("st6", c)])
                    OP("dve", (lambda c_, cn_: lambda e: e.bn_aggr(mv[0:cn_, c_, :], st6[0:cn_, c_, :]))(c, cn),
                       [("st6", c)], [("mv", c)])
                return vgj

            def ln(l, is_sample, sub, vgj):
                si, s0, sn, xj, q = sub[:5]
                is_sample = sub[6]
                cn = min(128, sn)
                nch = sn // cn
                mvk = [("mv", c) for c in range(nch)]
                act(sd[0:cn, 0:nch], mv[0:cn, 0:nch, 1], AF.Sqrt, mvk, ["sd"], bias=EPS)
                OP("dve", lambda e: e.reciprocal(out=sd[0:cn, 0:nch], in_=sd[0:cn, 0:nch]), ["sd"], ["sd"])
                for c in range(nch):
                    j = vgj[c]
                    ts(n_t[0:cn, c, :], vg[j][0:cn, :], mv[0:cn, c, 0:1], sd[0:cn, c:c + 1], ALU.subtract, ALU.mult,
                       [("vg", j), ("mv", c), "sd"], [("n", c)], z=True)
                if is_sample:
                    S.op("sp", lambda e: e.dma_start(out=af32[0:DEC, :], in_=gbrow_d[l:l + 1, 0:512].partition_broadcast(DEC)),
                         writes=["af32"], sem="gbld")
                    S.op("sp", lambda e: e.dma_start(out=rs_fin[0:DEC, :], in_=gbrow_d[l:l + 1, 512:1024].partition_broadcast(DEC)),
                         writes=["rsfin"], sem="gbld2")
                    j = vgj[0]
                    ts(vg[j][0:cn, :], vg[j][0:cn, :], mv[0:cn, 0, 0:1], sd[0:cn, 0:1], ALU.subtract, ALU.mult,
                       [("vg", j), ("mv", 0), "sd"], [("vg", j)])
                    tt(vg[j][0:cn, :], vg[j][0:cn, :], af32[0:cn, :], ALU.mult, [("vg", j), "af32"], [("vg", j)])
                    tt(vg[j][0:cn, :], vg[j][0:cn, :], rs_fin[0:cn, :], ALU.add, [("vg", j), "rsfin"], [("vg", j)])
                    store_ops.append(S.op("sp", lambda e: e.dma_start(out=vout[l, :, :], in_=vg[j][0:DEC, :]),
                                          reads=[("vg", j)], sem=("vst", l)))

            def spatial(l, sub):
                si, s0, sn, xj, q = sub[:5]
                cn = min(128, sn)
                nch = sn // cn
                for hh in range(4):
                    bk = rot("bank", nbk())
                    mm([(ps[bk][:, c * cn:(c + 1) * cn], n_t[0:cn, c, hh * 128:(hh + 1) * 128], WT[0:cn, l * 4 + hh, 0:cn], True, True)
                        for c in range(nch)], [("n", c) for c in range(nch)] + ["WT"], [("ps", bk)], z=True)
                    jt = rot("tf", NTF)
                    stt(tf[jt][:, :sn].rearrange("p (c t) -> p c t", c=nch),
                        ps[bk][:, :sn].rearrange("p (c t) -> p c t", c=nch), col(C_GV + l * 4 + hh),
                        Bt[:, l * 4 + hh, 0:cn].unsqueeze(1).to_broadcast([128, nch, cn]), ALU.mult, ALU.add,
                        [("ps", bk), "smallp", ("Bt", l)], [("tf", jt)])
                    tt(yab[:, 4 + hh, s0:s0 + sn], tf[jt][:, :sn], u_t[:, hh, s0:s0 + sn], ALU.mult,
                       [("tf", jt), ("u", hh, si)], [("yab", 4 + hh, si)], z=True)

            def o_group(l, b, sub, m, slot, blk):
                si, s0, sn, xj, q = sub[:5]
                b = sub[5]
                mq = m % 4
                bk = rot("bank", nbk())
                mm([(ps[bk][:, :sn], blk[:, k, mq * 128:(mq + 1) * 128], yab[:, k, s0:s0 + sn], k == 0, k == KD - 1)
                    for k in range(KD)], [("ring", slot)] + [("yab", k, si) for k in range(KD)], [("ps", bk)], z=True)
                stt(xs[xj][:, m, :sn], ps[bk][:, :sn], mod[:, l, 16 + m, b:b + 1], xs[xj][:, m, :sn],
                    ALU.mult, ALU.add, [("ps", bk), ("mod", l, 2), ("x", xj, m)], [("x", xj, m)])

            def ff1_group(sub, fc, mq, slot, blk):
                si, s0, sn, xj, q = sub[:5]
                bk = rot("bank", nbk())
                mm([(ps[bk][:, :sn], blk[:, k, mq * 128:(mq + 1) * 128], h[:, k, s0:s0 + sn], k == 0, k == KD - 1)
                    for k in range(KD)], [("ring", slot)] + [("h", k, si) for k in range(KD)], [("ps", bk)])
                jt = rot("tf", NTF)
                act(tf[jt][:, :sn], ps[bk][:, :sn], AF.Relu, [("ps", bk)], [("tf", jt)])
                tt(hid[:, fc, s0:s0 + sn], tf[jt][:, :sn], tf[jt][:, :sn], ALU.mult, [("tf", jt)], [("hid", fc, si)], z=True)

            def ff2_group(l, b, sub, m, mq, slot, blk):
                si, s0, sn, xj, q = sub[:5]
                b = sub[5]
                bk = rot("bank", nbk())
                mm([(ps[bk][:, :sn], blk[:, fc, mq * 128:(mq + 1) * 128], hid[:, fc, s0:s0 + sn], fc == 0, fc == 15)
                    for fc in range(16)], [("ring", slot)] + [("hid", fc, si) for fc in range(16)], [("ps", bk)], z=True)
                stt(xs[xj][:, m, :sn], ps[bk][:, :sn], mod[:, l, 40 + m, b:b + 1], xs[xj][:, m, :sn],
                    ALU.mult, ALU.add, [("ps", bk), ("mod", l, 5), ("x", xj, m)], [("x", xj, m)])

            TL = [(ti, l) for ti in range(len(tiles)) for l in range(DEPTH)]
            first = tile_subs[0][0]
            n_sq(first)
            n_rest(first, 0, 1, tiles[0][0])
            pending_fin = None
            for i, (ti, l) in enumerate(TL):
                b, t0, nt, g0 = tiles[ti]
                is_sample = (b == 2)
                subs = tile_subs[ti]
                nsub = len(subs)
                sA = subs[0]
                sB = subs[1] if nsub > 1 else None
                sC = subs[2] if nsub > 2 else None

                def normC(which):
                    if sC is not None:
                        n_rest(sC, l, which, b)
                new_tile = (l == 0)
                zphase()
                if sB is not None and not (new_tile and pending_fin is not None):
                    n_sq(sB)
                if sC is not None:
                    n_sq(sC)
                bA = blk_in(l, 0)
                bU = blk_in(l, 1)
                bV = blk_in(l, 2)
                a_groups(l, b, t0, is_sample, nsub, sA, *bA)
                fin_late = None
                if new_tile and pending_fin is not None:
                    fin_stats_def(pending_fin[0])
                    fin_late = pending_fin[0]
                    pending_fin = None
                    if sB is not None:
                        n_sq(sB)
                    pooling(l, t0, is_sample, sA)
                    u_groups(sA, *bU, ms=(0, 1))
                    if sB is not None:
                        n_rest(sB, l, 1, b)
                    normC(1)
                    u_groups(sA, *bU, ms=(2, 3))
                else:
                    pooling(l, t0, is_sample, sA)
                    if sB is not None:
                        n_rest(sB, l, 1, b)
                    normC(1)
                    u_groups(sA, *bU)
                vgA = v_chunks(sA, *bV)
                ln(l, is_sample, sA, vgA)
                if sB is not None:
                    a_groups(l, b, t0, is_sample, nsub, sB, *bA)
                    if sC is None:
                        release(bA)
                    spatial(l, sA)
                    if fin_late is not None:
                        fin_chain_def(fin_late)
                    pooling(l, t0, is_sample, sB)
                    u_groups(sB, *bU)
                    ya_groups(l, sA)
                    vgB = v_chunks(sB, *bV)
                    ln(l, is_sample, sB, vgB)
                    if sC is not None:
                        a_groups(l, b, t0, is_sample, nsub, sC, *bA)
                        release(bA)
                        u_groups(sC, *bU)
                        vgC = v_chunks(sC, *bV)
                    release(bU)
                    release(bV)
                else:
                    release(bA)
                    release(bU)
                    release(bV)
                    ya_groups(l, sA)
                    spatial(l, sA)
                if ti == 0 and l == 0:
                    for _ in range(6):
                        mod_step()
                bO = [blk_out(l, 0), blk_out(l, 1)]
                for m in range(4):
                    o_group(l, b, sA, m, *bO[0])
                if sB is not None:
                    spatial(l, sB)
                if sC is not None:
                    pooling(l, t0, is_sample, sC)
                    ln(l, is_sample, sC, vgC)
                for m in range(4, 8):
                    o_group(l, b, sA, m, *bO[1])
                    if m == 5 and sB is not None:
                        ya_groups(l, sB)
                if sC is not None:
                    ya_groups(l, sC)
                    spatial(l, sC)
                n_sq(sA)
                if sB is not None:
                    for m in range(8):
                        o_group(l, b, sB, m, *bO[m // 4])
                        if m == 1:
                            n_rest(sA, l, 2, b)
                    if sC is not None:
                        for m in range(8):
                            o_group(l, b, sC, m, *bO[m // 4])
                    release(bO[0])
                    release(bO[1])
                    n_sq(sB)
                    if sC is not None:
                        n_sq(sC)
                else:
                    release(bO[0])
                    release(bO[1])
                    n_rest(sA, l, 2, b)
                for half in range(2):
                    zphase()
                    f1 = [blk_ff1(l, half, 0), blk_ff1(l, half, 1)]
                    if half == 0:
                        for fc in range(8):
                            ff1_group(sA, fc, fc % 4, *f1[fc // 4])
                            if fc == 1 and sB is not None:
                                n_rest(sB, l, 2, b)
                                normC(2)
                        if sB is not None:
                            for fc in range(8):
                                ff1_group(sB, fc, fc % 4, *f1[fc // 4])
                        if sC is not None:
                            for fc in range(8):
                                ff1_group(sC, fc, fc % 4, *f1[fc // 4])
                        release(f1[0])
                        release(f1[1])
                        if ti == 0 and l == 0:
                            mod_step()
                            mod_step()
                    else:
                        for fb in range(2):
                            for sub in subs:
                                for mq in range(4):
                                    ff1_group(sub, fb * 4 + mq, mq, *f1[fb])
                            release(f1[fb])
                        if ti == 0 and l == 0:
                            mod_step()
                            mod_step()
                    for fb in range(2, 4):
                        bf = blk_ff1(l, half, fb)
                        for sub in subs:
                            for mq in range(4):
                                ff1_group(sub, fb * 4 + mq, mq, *bf)
                        release(bf)
                        if ti == 0 and l == 0:
                            mod_step()
                    last_half = (half == 1)
                    nblk_major = 4 if not last_half else 1
                    for mb in range(nblk_major):
                        bf = blk_ff2(l, half, mb)
                        for sub in subs:
                            for mq in range(2):
                                ff2_group(l, b, sub, mb * 2 + mq, mq, *bf)
                        release(bf)
                        if ti == 0 and l == 0:
                            mod_step()
                    if last_half:
                        if ti == 0 and l == 0:
                            while mod_jobs:
                                mod_step()
                        f2 = [blk_ff2(l, half, 1), blk_ff2(l, half, 2), blk_ff2(l, half, 3)]
                        for m in range(2, 8):
                            ff2_group(l, b, sA, m, m % 2, *f2[(m - 2) // 2])
                        last_layer = (l == DEPTH - 1)
                        nxt = TL[i + 1] if i + 1 < len(TL) else None

                        def prologue():
                            if last_layer:
                                fin_rest(sA, sA[9])
                                if nxt is not None:
                                    nsub0 = tile_subs[nxt[0]][0]
                                    n_sq(nsub0)
                                    n_rest(nsub0, 0, 1, tiles[nxt[0]][0])
                            else:
                                n_rest(sA, l + 1, 1, b)

                        n_sq(sA)
                        if sB is not None:
                            for m in range(2, 8):
                                ff2_group(l, b, sB, m, m % 2, *f2[(m - 2) // 2])
                                if m % 2 == 1 and sC is None:
                                    release(f2[(m - 2) // 2])
                                if m == 3:
                                    prologue()
                            if sC is not None:
                                for m in range(2, 8):
                                    ff2_group(l, b, sC, m, m % 2, *f2[(m - 2) // 2])
                                for fb_ in f2:
                                    release(fb_)
                            if last_layer:
                                n_sq(sB)
                                if nxt is not None:
                                    pending_fin = (sB, sB[9])
                                else:
                                    fin_rest(sB, sB[9])
                                    if sC is not None:
                                        n_sq(sC)
                                        fin_rest(sC, sC[9])
                        else:
                            for fb_ in f2:
                                release(fb_)
                            prologue()
            S.op("sp", None, after=store_ops)
            return collected

        plan = record(Sched(), None)
        S = Sched()
        record(S, plan)
        S.emit(nc)
    return nc


def _fm(v, nchunk):
    return np.ascontiguousarray(v.reshape(nchunk, 128).T)


def kernel(x_prompt, x_sample, state_pool, c_prompt, c_sample, w_ada, b_ada, norm_mix_g, w_in,
           w_pool, pool_scale, v_norm_g, v_norm_b, w_spatial, b_spatial, w_out, norm_ffn_g,
           w_ff1, w_ff2, final_norm_g):
    f32 = np.float32
    A = lambda a: np.ascontiguousarray(np.asarray(a, dtype=f32))
    x_prompt, x_sample, state_pool = A(x_prompt), A(x_sample), A(state_pool)
    c_prompt, c_sample = A(c_prompt), A(c_sample)
    w_ada, w_in, w_out, w_ff1, w_ff2 = A(w_ada), A(w_in), A(w_out), A(w_ff1), A(w_ff2)
    b_ada, norm_mix_g, norm_ffn_g, final_norm_g = A(b_ada), A(norm_mix_g), A(norm_ffn_g), A(final_norm_g)
    w_pool, pool_scale, v_norm_g, v_norm_b = A(w_pool), A(pool_scale), A(v_norm_g), A(v_norm_b)
    w_spatial, b_spatial = A(w_spatial), A(b_spatial)

    shared_small = np.zeros((128, NSMALL), f32)
    for l in range(DEPTH):
        shared_small[:, C_BADA + l * 48:C_BADA + (l + 1) * 48] = _fm(b_ada[l], 48)
        shared_small[:, C_GMIX + l * 8:C_GMIX + (l + 1) * 8] = _fm(norm_mix_g[l], 8)
        shared_small[:, C_GFFN + l * 8:C_GFFN + (l + 1) * 8] = _fm(norm_ffn_g[l], 8)
        shared_small[:, C_PSC + l * 4:C_PSC + (l + 1) * 4] = _fm(pool_scale[l], 4)
        shared_small[:, C_GV + l * 4:C_GV + (l + 1) * 4] = _fm(v_norm_g[l], 4)
        shared_small[:, C_BV + l * 4:C_BV + (l + 1) * 4] = _fm(v_norm_b[l], 4)
    shared_small[:, C_GFIN:C_GFIN + 8] = _fm(final_norm_g, 8)
    for g, w in enumerate(POOL_W):
        for p in range(16):
            shared_small[:, C_INV + g * 16 + p] = 1.0 / min(p + 1, w)
    bsrow = np.ascontiguousarray(b_spatial.reshape(DEPTH, 512))
    gbrow = np.ascontiguousarray(np.concatenate([v_norm_g, v_norm_b], axis=1))
    maskt = np.ascontiguousarray(np.triu(np.ones((128, 128), f32)))
    wspT = np.ascontiguousarray(w_spatial.transpose(3, 0, 1, 2).reshape(128, DEPTH * 4 * 128))
    wpool_l = np.ascontiguousarray(w_pool.transpose(2, 0, 1, 3).reshape(128, DEPTH * 4 * 128))

    tt_ = np.arange(128)[None, :]
    ss_ = np.arange(128)[:, None]
    dwin = np.zeros((128, 3 * 4 * 128 + 4 * 64), f32)
    for g, w in enumerate(POOL_W):
        eye = (ss_ == tt_).astype(f32)
        dm = ((ss_ <= tt_) & (ss_ > tt_ - w)).astype(f32) / w - eye
        dh = (ss_ > tt_ - w + 128).astype(f32) / w
        cnt = np.minimum(tt_ + 1, w).astype(f32)
        dm0 = ((ss_ <= tt_) & (ss_ > tt_ - w)).astype(f32) / cnt - eye
        dhs = (ss_[:, :] > tt_[:, :64] - w + 15).astype(f32) / w
        dhs[15:, :] = 0.0
        dwin[:, 0 + g * 128:0 + (g + 1) * 128] = dm
        dwin[:, 512 + g * 128:512 + (g + 1) * 128] = dh
        dwin[:, 1024 + g * 128:1024 + (g + 1) * 128] = dm0
        dwin[:, 1536 + g * 64:1536 + (g + 1) * 64] = dhs

    in_maps = []
    for i in range(NCORES):
        xt = np.concatenate([x_prompt[2 * i], x_prompt[2 * i + 1], x_sample[i]], axis=0)
        x_fm = np.ascontiguousarray(xt.reshape(NTOK, KD, 128).transpose(2, 1, 0))
        sp = shared_small.copy()
        cs = np.stack([c_prompt[2 * i], c_prompt[2 * i + 1], c_sample[i]], axis=0)
        sp[:, C_C:C_C + 24] = cs.reshape(3, KD, 128).transpose(2, 1, 0).reshape(128, 24)
        in_maps.append({
            "x_fm": x_fm, "smallp": sp, "bsrow": bsrow, "gbrow": gbrow, "maskt": maskt, "wspT": wspT,
            "wpool_l": wpool_l, "statein": np.ascontiguousarray(state_pool[:, i]), "dwin": dwin,
            "w_ada": w_ada, "w_in": w_in, "w_out": w_out, "w_ff1": w_ff1, "w_ff2": w_ff2,
        })
    nc = build_nc()
    res = run_bass_kernel_spmd(nc, in_maps, core_ids=list(range(NCORES)))

    y_prompt = np.empty((16, SEQ, D), f32)
    y_sample = np.empty((8, DEC, D), f32)
    sp_prompt = np.empty((DEPTH, 16, 15, 512), f32)
    sp_sample = np.empty((DEPTH, 8, 15, 512), f32)
    sv = np.empty((DEPTH, 8, DEC, 512), f32)
    for i in range(NCORES):
        r = res.results[i]
        yt = np.asarray(r["y_fm"]).transpose(2, 1, 0).reshape(NTOK, D)
        y_prompt[2 * i] = yt[0:SEQ]
        y_prompt[2 * i + 1] = yt[SEQ:2 * SEQ]
        y_sample[i] = yt[2 * SEQ:]
        po = np.asarray(r["pool_out"])
        sp_prompt[:, 2 * i] = po[0]
        sp_prompt[:, 2 * i + 1] = po[1]
        sp_sample[:, i] = po[2]
        sv[:, i] = np.asarray(r["vout"])
    return (y_prompt, y_sample, sp_prompt, sp_sample, sv)
```

```python
import os
import sys
import numpy as np
from contextlib import ExitStack
import concourse.bass as bass
import concourse.mybir as mybir
from concourse.bass_utils import run_bass_kernel_spmd

F32 = mybir.dt.float32
BF16 = mybir.dt.bfloat16
AF = mybir.ActivationFunctionType
ALU = mybir.AluOpType

NCORES = 8
D = 1024
KD = 8
DEPTH = 2
SEQ = 4096
DEC = 64
NTOK = 2 * SEQ + DEC
TILE = 1024
TILEW = TILE + 64
SUB = 512
EPS = 1e-6
RING = 5
NTF = 5
NBANK = 7
POOL_W = (2, 4, 8, 16)

C_C = 0
C_BADA = 24
C_GMIX = 120
C_GFFN = 136
C_GFIN = 152
C_PSC = 160
C_GV = 168
C_BV = 176
C_INV = 184
NSMALL = 248


_KDEBUG = bool(os.environ.get("KDEBUG"))
_KTAGS = {}


class Op:
    __slots__ = ("eng", "fn", "deps", "signal", "sem", "token", "idx", "is_dma", "tag")

    def __init__(self, eng, fn, idx, sem=None):
        self.eng = eng
        self.fn = fn
        self.deps = []
        self.signal = False
        self.sem = sem
        self.token = None
        self.idx = idx
        self.is_dma = sem is not None


class Sched:
    ENGS = ("pe", "act", "dve", "pool", "sp")

    def __init__(self):
        self.ops = []
        self.last_w = {}
        self.readers = {}
        self.total_wait = set()

    def op(self, eng, fn, reads=(), writes=(), after=(), sem=None):
        o = Op(eng, fn, len(self.ops), sem)
        o.tag = None
        if _KDEBUG:
            f = sys._getframe(1)
            names = []
            while f is not None and len(names) < 6:
                nm = f.f_code.co_name
                if nm not in ("OP", "act", "tt", "ts", "stt", "cp", "mm", "<lambda>"):
                    names.append("%s:%d" % (nm, f.f_lineno))
                f = f.f_back
            o.tag = "|".join(names[:3])
        deps = {}
        for k in reads:
            w = self.last_w.get(k)
            if w is not None:
                deps[w.idx] = w
        for k in writes:
            w = self.last_w.get(k)
            if w is not None:
                deps[w.idx] = w
            for r in self.readers.get(k, {}).values():
                if isinstance(r, list):
                    for rr in r:
                        deps[rr.idx] = rr
                else:
                    deps[r.idx] = r
        for a in after:
            if a is not None:
                deps[a.idx] = a
        for k in reads:
            d = self.readers.setdefault(k, {})
            if o.is_dma:
                d.setdefault("dma", []).append(o)
            else:
                d[eng] = o
        for k in writes:
            self.last_w[k] = o
            self.readers[k] = {}
        o.deps = list(deps.values())
        for d in o.deps:
            d.signal = True
        self.ops.append(o)
        return o

    def emit(self, nc):
        semkeys = list(self.ENGS[:4])
        for o in self.ops:
            if o.is_dma and o.sem not in semkeys:
                semkeys.append(o.sem)
        counters = {k: 0 for k in semkeys}
        for o in self.ops:
            if o.is_dma:
                counters[o.sem] += 16
                o.token = (o.sem, counters[o.sem])
            elif o.signal:
                counters[o.eng] += 1
                o.token = (o.eng, counters[o.eng])
        for o in self.ops:
            if o.is_dma and o.sem in self.total_wait:
                o.token = (o.sem, counters[o.sem])
        per_eng = {e: [o for o in self.ops if o.eng == e] for e in self.ENGS}
        with ExitStack() as es:
            sems = {k: es.enter_context(nc.semaphore("s_%d" % i)) for i, k in enumerate(semkeys)}
            block = es.enter_context(nc.Block())

            def run(eng_name):
                def body(h):
                    waited = {}
                    for o in per_eng[eng_name]:
                        for d in o.deps:
                            if (not d.is_dma) and d.eng == "pe" and eng_name == "pe":
                                continue
                            sk, val = d.token
                            if waited.get(sk, 0) >= val:
                                continue
                            waited[sk] = val
                            h.wait_ge(sems[sk], val)
                        ins = o.fn(h) if o.fn is not None else None
                        if _KDEBUG and ins is not None:
                            _KTAGS[str(ins.ins.name)] = o.tag
                        if o.is_dma:
                            ins.then_inc(sems[o.sem], 16)
                        elif o.signal:
                            ins.then_inc(sems[o.eng], 1)
                return body

            block.tensor(run("pe"))
            block.scalar(run("act"))
            block.vector(run("dve"))
            block.gpsimd(run("pool"))
            block.sync(run("sp"))


def build_nc():
    nc = bass.Bass("TRN2", target_bir_lowering=False)
    dt_in = lambda n, s: nc.dram_tensor(n, s, F32, kind="ExternalInput").ap()
    dt_out = lambda n, s: nc.dram_tensor(n, s, F32, kind="ExternalOutput").ap()
    x_fm = dt_in("x_fm", [128, KD, NTOK])
    smallp_d = dt_in("smallp", [128, NSMALL])
    bsrow_d = dt_in("bsrow", [DEPTH, 512])
    gbrow_d = dt_in("gbrow", [DEPTH, 1024])
    maskt_d = dt_in("maskt", [128, 128])
    wspT_d = dt_in("wspT", [128, DEPTH * 4 * 128])
    wpool_d = dt_in("wpool_l", [128, DEPTH * 4 * 128])
    statein_d = dt_in("statein", [DEPTH, 15, 512])
    dwin_d = dt_in("dwin", [128, 3 * 4 * 128 + 4 * 64])
    w_ada = dt_in("w_ada", [DEPTH, D, 6 * D])
    w_in = dt_in("w_in", [DEPTH, D, 1536])
    w_out = dt_in("w_out", [DEPTH, D, D])
    w_ff1 = dt_in("w_ff1", [DEPTH, D, 4 * D])
    w_ff2 = dt_in("w_ff2", [DEPTH, 4 * D, D])
    y_fm = dt_out("y_fm", [128, KD, NTOK])
    pool_out = dt_out("pool_out", [3, DEPTH, 15, 512])
    vout = dt_out("vout", [DEPTH, DEC, 512])

    with ExitStack() as es:
        sb = lambda n, s, d: es.enter_context(nc.sbuf_tensor(n, s, d))
        xs = [sb("xs%d" % j, [128, KD, SUB], F32) for j in range(3)] + [sb("xs3", [128, KD, DEC], F32)]
        h = sb("h", [128, KD, TILEW], BF16)
        Z = sb("Z", [128, 16 * TILEW], BF16)
        hid = Z[:, :].rearrange("p (f t) -> p f t", f=16)
        yab = Z[:, 0:8 * TILEW].rearrange("p (k t) -> p k t", k=8)
        u_t = Z[:, 8 * TILEW:12 * TILEW].rearrange("p (k t) -> p k t", k=4)
        n_t = Z[:, 12 * TILEW:12 * TILEW + 2048].rearrange("p (c n) -> p c n", c=4)
        d_t = Z[:, 12 * TILEW + 2048:12 * TILEW + 4096].rearrange("p (g t) -> p g t", g=4)
        sq = [sb("sq%d" % j, [128, SUB], BF16) for j in range(KD)]
        rs = [sb("rs%d" % j, [128, SUB], F32) for j in range(2)]
        rs_fin = sb("rs_fin", [128, SUB], F32)
        sqc = [sb("sqc%d" % j, [128, DEC], BF16) for j in range(KD)]
        tf = [sb("tf%d" % j, [128, SUB], F32) for j in range(NTF)]
        vg = [sb("vg%d" % j, [128, 512], F32) for j in range(4)]
        atm = [sb("atm%d" % j, [128, 512], BF16) for j in range(4)]
        akeep = sb("akeep", [128, DEPTH, 512], BF16)
        statebf = sb("statebf", [128, DEPTH, 512], BF16)
        af32 = sb("af32", [128, 512], F32)
        Dw = sb("Dw", [128, 3 * 4 * 128 + 4 * 64], BF16)
        st6 = sb("st6", [128, 4, 6], F32)
        mv = sb("mv", [128, 4, 2], F32)
        sd = sb("sd", [128, 4], F32)
        d_t2 = sb("d_t2", [128, 4, SUB], BF16)
        ring = [sb("ring%d" % j, [128, 4096], BF16) for j in range(RING)]
        smallp = sb("smallp_sb", [128, NSMALL], F32)
        scb = sb("scb", [128, KD, 3], BF16)
        mod = sb("mod", [128, DEPTH, 48, 3], F32)
        gm1 = sb("gm1", [128, DEPTH, KD, 3], F32)
        gm2 = sb("gm2", [128, DEPTH, KD, 3], F32)
        WTraw = sb("WTraw", [128, DEPTH * 4, 128], BF16)
        WT = sb("WT", [128, DEPTH * 4, 128], BF16)
        maskb = sb("maskb", [128, 128], BF16)
        Bt = sb("Bt", [128, DEPTH * 4, 128], F32)
        wpool = sb("wpool", [128, DEPTH * 4, 128], BF16)
        ones = sb("ones", [128, 128], BF16)
        ps = [es.enter_context(nc.psum_tensor("ps%d" % i, [128, 512], F32)) for i in range(8)]

        def record(S, plan):
            collected = []
            cnt = {"bank": 0, "rs": 0, "tf": 0, "vg": 0}

            def rot(name, n):
                v = cnt[name] % n
                cnt[name] += 1
                return v

            zstate = {"prev": {}, "cur": {}}

            def zphase():
                zstate["prev"] = zstate["cur"]
                zstate["cur"] = {}

            def OP(eng, fn, reads=(), writes=(), z=False, after=(), sem=None):
                aft = list(after)
                if z:
                    aft += list(zstate["prev"].values())
                o = S.op(eng, fn, reads, writes, aft, sem)
                if z:
                    zstate["cur"][eng] = o
                return o

            def act(out, in_, func, reads, writes, bias=0.0, scale=1.0, z=False):
                return OP("act", lambda e: e.activation(out=out, in_=in_, func=func, bias=bias, scale=scale),
                          reads, writes, z)

            def tt(out, in0, in1, op, reads, writes, z=False, eng="dve"):
                return OP(eng, lambda e: e.tensor_tensor(out=out, in0=in0, in1=in1, op=op), reads, writes, z)

            def ts(out, in0, s1, s2, op0, op1, reads, writes, z=False, eng="dve"):
                if op1 is None:
                    return OP(eng, lambda e: e.tensor_scalar(out, in0, s1, None, op0), reads, writes, z)
                return OP(eng, lambda e: e.tensor_scalar(out, in0, s1, s2, op0, op1), reads, writes, z)

            def stt(out, in0, scalar, in1, op0, op1, reads, writes, z=False, eng="dve"):
                return OP(eng, lambda e: e.scalar_tensor_tensor(out=out, in0=in0, scalar=scalar, in1=in1,
                                                                op0=op0, op1=op1), reads, writes, z)

            def cp(out, in_, reads, writes, z=False, eng="dve"):
                return OP(eng, lambda e: e.tensor_copy(out=out, in_=in_), reads, writes, z)

            def mm(mms, reads, writes, z=False):
                def fn(e):
                    ins = None
                    for (o_, l_, r_, st_, sp_) in mms:
                        ins = e.matmul(o_, l_, r_, start=st_, stop=sp_)
                    return ins
                return OP("pe", fn, reads, writes, z)

            def col(c):
                return smallp[:, c:c + 1]

            nbk_box = [lambda: NBANK]

            def nbk():
                return nbk_box[0]()

            bstate = {"n": 0, "next_load": 0}
            slot_idx = {}

            def rec_load(i, src, kk):
                slot = i % RING
                dst = ring[slot][:, :].rearrange("p (k n) -> p k n", k=kk)
                S.op("pool", lambda e: e.dma_start(out=dst, in_=src), writes=[("ring", slot)], sem=("ring", slot))

            def acquire(src, kk):
                i = bstate["n"]
                bstate["n"] += 1
                slot = i % RING
                slot_idx[slot] = i
                if plan is None:
                    collected.append((src, kk))
                    rec_load(i, src, kk)
                else:
                    assert i < bstate["next_load"], "ring block used before its load was recorded"
                return slot, ring[slot][:, :].rearrange("p (k n) -> p k n", k=kk)

            def release(b):
                if plan is None:
                    return
                j = slot_idx[b[0]] + RING
                assert j == bstate["next_load"], "blocks must be released in acquisition order"
                if j < len(plan):
                    rec_load(j, *plan[j])
                bstate["next_load"] = j + 1

            def blk_in(l, i):
                return acquire(w_in[l, :, i * 512:(i + 1) * 512].rearrange("(k p) n -> p k n", p=128), 8)

            def blk_out(l, i):
                return acquire(w_out[l, :, i * 512:(i + 1) * 512].rearrange("(k p) n -> p k n", p=128), 8)

            def blk_ff1(l, half, fb):
                c0 = (half * 4 + fb) * 512
                return acquire(w_ff1[l, :, c0:c0 + 512].rearrange("(k p) n -> p k n", p=128), 8)

            def blk_ff2(l, half, mb):
                return acquire(w_ff2[l, half * 2048:(half + 1) * 2048, mb * 256:(mb + 1) * 256]
                               .rearrange("(f p) n -> p f n", p=128), 16)

            S.total_wait.add("setup")
            SPL = lambda out, in_, w: S.op("sp", lambda e: e.dma_start(out=out, in_=in_), writes=w, sem="setup")
            SPL(smallp[:, :], smallp_d[:, :], ["smallp"])
            if plan is not None:
                for i0 in range(min(2, len(plan))):
                    rec_load(i0, *plan[i0])
            S.total_wait.add("setupc")
            PLL = lambda out, in_, w: S.op("pool", lambda e: e.dma_start(out=out, in_=in_), writes=w, sem="setupc")
            PLL(WTraw[:, :, :], wspT_d.rearrange("p (a t) -> p a t", t=128), ["WTraw"])
            PLL(wpool[:, :, :], wpool_d.rearrange("p (a t) -> p a t", t=128), ["wpool"])
            PLL(maskb[:, :], maskt_d[:, :], ["maskb"])
            PLL(Dw[:, :], dwin_d[:, :], ["Dw"])
            PLL(statebf[0:15, :, :], statein_d.rearrange("l r c -> r l c"), ["statebf"])
            for l in range(DEPTH):
                SPL(tf[l][:, :], bsrow_d[l:l + 1, :].partition_broadcast(128), [("tf", l)])
            if plan is not None:
                for i0 in range(min(2, len(plan)), min(RING, len(plan))):
                    rec_load(i0, *plan[i0])
                bstate["next_load"] = min(RING, len(plan))
            OP("dve", lambda e: e.memset(ones[:, :], 1.0), writes=["ones"])
            tt(WT[:, :, :], WTraw[:, :, :], maskb[:, :].unsqueeze(1).to_broadcast([128, DEPTH * 4, 128]), ALU.mult,
               ["WTraw", "maskb"], ["WT"])
            act(scb[:, :, :], smallp[:, C_C:C_C + 24].rearrange("p (k b) -> p k b", b=3), AF.Silu, ["smallp"], ["scb"])
            MB = 7

            mod_ops = []

            def mod_job(l, j, hb):
                c0 = j * 1024 + hb * 512
                slot, blk = acquire(w_ada[l, :, c0:c0 + 512].rearrange("(k p) n -> p k n", p=128), 8)
                mms = []
                for m in range(4):
                    cc = (j * 8 + hb * 4 + m) * 3
                    for k in range(KD):
                        mms.append((ps[MB][:, cc:cc + 3], blk[:, k, m * 128:(m + 1) * 128], scb[:, k, :],
                                    k == 0, k == KD - 1))
                mod_ops.append(mm(mms, [("ring", slot), "scb"], [("ps", MB)]))
                release((slot, blk))

            def mod_finish_kind(l, j):
                tt(mod[:, l, j * 8:(j + 1) * 8, :], ps[MB][:, j * 24:(j + 1) * 24].rearrange("p (j b) -> p j b", b=3),
                   smallp[:, C_BADA + l * 48 + j * 8:C_BADA + l * 48 + (j + 1) * 8].unsqueeze(2).to_broadcast([128, 8, 3]),
                   ALU.add, [("ps", MB), "smallp"], [("mod", l, j)])
                for (gm, jj, cg) in ((gm1, 1, C_GMIX), (gm2, 4, C_GFFN)):
                    if j == jj:
                        ts(gm[:, l, :, :], mod[:, l, j * 8:(j + 1) * 8, :], 1.0, None, ALU.add, None, [("mod", l, j)], [("gm", l, j * 8)])
                        tt(gm[:, l, :, :], gm[:, l, :, :],
                           smallp[:, cg + l * 8:cg + (l + 1) * 8].unsqueeze(2).to_broadcast([128, 8, 3]), ALU.mult,
                           [("gm", l, j * 8), "smallp"], [("gm", l, j * 8)])

            mod_jobs = [(l_, j_, hb_) for l_ in range(DEPTH) for j_ in range(6) for hb_ in range(2)]
            nbk_box[0] = lambda: (NBANK if mod_jobs else 8)

            def mod_step():
                if mod_jobs:
                    l_, j_, hb_ = mod_jobs.pop(0)
                    mod_job(l_, j_, hb_)
                    if hb_ == 1:
                        mod_finish_kind(l_, j_)

            for _ in range(4):
                mod_step()
            for l in range(DEPTH):
                bk = rot("bank", nbk())
                mm([(ps[bk][:, hh * 128:(hh + 1) * 128], ones[:, :], WT[:, l * 4 + hh, :], True, True) for hh in range(4)],
                   ["ones", "WT"], [("ps", bk)])
                for hh in range(4):
                    stt(Bt[:, l * 4 + hh, :], ps[bk][:, hh * 128:(hh + 1) * 128], col(C_BV + l * 4 + hh),
                        tf[l][:, hh * 128:(hh + 1) * 128], ALU.mult, ALU.add,
                        [("ps", bk), "smallp", ("tf", l)], [("Bt", l)])


            tiles = []
            for b in range(2):
                for t0 in range(0, SEQ, TILE):
                    tiles.append((b, t0, TILE, b * SEQ + t0))
            subs_all = []
            tile_subs = []
            for ti, (b, t0, nt, g0) in enumerate(tiles):
                lst = []
                for si, s0 in enumerate(range(0, nt, SUB)):
                    q = len(subs_all)
                    subs_all.append((ti, s0, SUB, g0 + s0))
                    lst.append((si, s0, SUB, q % 3, q, b, False, t0 + s0 == 0, t0 + s0 + SUB == SEQ, g0 + s0))
                tile_subs.append(lst)
            tile_subs[-1].append((2, TILE, DEC, 3, None, 2, True, False, True, 2 * SEQ))
            store_ops = []

            def xload(q, after=()):
                if q >= len(subs_all):
                    return
                ti, s0, sn, gg = subs_all[q]
                j = q % 3
                S.op("sp", lambda e: e.dma_start(out=xs[j][:, :, 0:sn], in_=x_fm[:, :, gg:gg + sn]),
                     writes=[("x", j, k) for k in range(KD)], after=after, sem=("xld", j))

            xload(0)
            xload(1, after=mod_ops[1:2])
            xload(2, after=mod_ops[-1:])
            S.op("sp", lambda e: e.dma_start(out=xs[3][:, :, :], in_=x_fm[:, :, 2 * SEQ:2 * SEQ + DEC]),
                 writes=[("x", 3, k) for k in range(KD)], after=mod_ops[-1:], sem=("xld", 3))

            def sqbuf(sub):
                return (sqc, "sqc") if sub[6] else (sq, "sq")

            def n_sq(sub):
                si, s0, sn, xj, q = sub[:5]
                sqb, sqk = sqbuf(sub)
                for k in range(KD):
                    act(sqb[k][:, :sn], xs[xj][:, k, :sn], AF.Square, [("x", xj, k)], [(sqk, k)])

            def n_stats_pe(sub):
                si, s0, sn, xj, q = sub[:5]
                sqb, sqk = sqbuf(sub)
                bk = rot("bank", nbk())
                for k in range(KD):
                    mm([(ps[bk][:, :sn], ones[:, :], sqb[k][:, :sn], k == 0, k == KD - 1)], [(sqk, k), "ones"], [("ps", bk)])
                return bk

            def fin_stats_def(sub):
                si, s0, sn, xj, q = sub[:5]
                bk = n_stats_pe(sub)
                act(rs_fin[:, :sn], ps[bk][:, :sn], AF.Sqrt, [("ps", bk)], ["rsfin"], bias=EPS, scale=1.0 / D)

            def fin_chain_def(sub):
                si, s0, sn, xj, q = sub[:5]
                gg = sub[9]
                x = xs[xj]
                OP("dve", lambda e: e.reciprocal(out=rs_fin[:, :sn], in_=rs_fin[:, :sn]), ["rsfin"], ["rsfin"])
                for k in range(KD):
                    stt(x[:, k, :sn], x[:, k, :sn], col(C_GFIN + k), rs_fin[:, :sn], ALU.mult, ALU.mult,
                        [("x", xj, k), "rsfin", "smallp"], [("x", xj, k)])
                store_ops.append(S.op("sp", lambda e: e.dma_start(out=y_fm[:, :, gg:gg + sn], in_=xs[xj][:, :, 0:sn]),
                                      reads=[("x", xj, k) for k in range(KD)], sem=("xst", xj)))
                if q is not None:
                    xload(q + 3)

            def n_stats(sub):
                si, s0, sn, xj, q = sub[:5]
                bk = n_stats_pe(sub)
                jr = rot("rs", 2)
                act(rs[jr][:, :sn], ps[bk][:, :sn], AF.Sqrt, [("ps", bk)], [("rs", jr)], bias=EPS, scale=1.0 / D)
                OP("dve", lambda e: e.reciprocal(out=rs[jr][:, :sn], in_=rs[jr][:, :sn]), [("rs", jr)], [("rs", jr)])
                return jr

            def n_chain(sub, jr, l, which, b):
                si, s0, sn, xj, q = sub[:5]
                b = sub[5]
                x = xs[xj]
                for k in range(KD):
                    if which == 0:
                        stt(x[:, k, :sn], x[:, k, :sn], col(C_GFIN + k), rs[jr][:, :sn], ALU.mult, ALU.mult,
                            [("x", xj, k), ("rs", jr), "smallp"], [("x", xj, k)])
                    else:
                        gm = gm1 if which == 1 else gm2
                        j0 = 8 if which == 1 else 32
                        shj = 0 if which == 1 else 24
                        jt = rot("tf", NTF)
                        tt(tf[jt][:, :sn], x[:, k, :sn], rs[jr][:, :sn], ALU.mult, [("x", xj, k), ("rs", jr)], [("tf", jt)],
                           eng="dve")
                        act(h[:, k, s0:s0 + sn], tf[jt][:, :sn], AF.Identity, [("tf", jt), ("gm", l, j0), ("mod", l, 0 if which == 1 else 3)],
                            [("h", k, si)], bias=mod[:, l, shj + k, b:b + 1], scale=gm[:, l, k, b:b + 1])

            def n_rest(sub, l, which, b):
                jr = n_stats(sub)
                n_chain(sub, jr, l, which, b)

            def fin_rest(sub, gg):
                si, s0, sn, xj, q = sub[:5]
                gg = sub[9]
                n_rest(sub, 0, 0, 0)
                store_ops.append(S.op("sp", lambda e: e.dma_start(out=y_fm[:, :, gg:gg + sn], in_=xs[xj][:, :, 0:sn]),
                                      reads=[("x", xj, k) for k in range(KD)], sem=("xst", xj)))
                if q is not None:
                    xload(q + 3)

            def a_groups(l, b, t0, is_sample, nsub, sub, slot, blk):
                si, s0, sn, xj, q = sub[:5]
                b, is_sample, seq_start, seq_end = sub[5:9]
                cn = min(128, sn)
                nch = sn // cn
                for c in range(nch):
                    bk = rot("bank", nbk())
                    mm([(ps[bk][0:cn, :], h[:, k, s0 + c * cn:s0 + (c + 1) * cn], blk[:, k, :], k == 0, k == KD - 1)
                        for k in range(KD)], [("ring", slot)] + [("h", k, si) for k in range(KD)], [("ps", bk)])
                    act(atm[c][0:cn, :], ps[bk][0:cn, :], AF.Identity, [("ps", bk)], [("atm", c)])
                    if seq_end and c == nch - 1:
                        act(af32[0:cn, :], ps[bk][0:cn, :], AF.Identity, [("ps", bk)], ["af32"])
                        store_ops.append(S.op("sp", lambda e: e.dma_start(out=pool_out[b, l, :, :], in_=af32[cn - 15:cn, :]),
                                              reads=["af32"], sem=("pst", b, l)))

            def pooling(l, t0, is_sample, sub):
                si, s0, sn, xj, q = sub[:5]
                b, is_sample, seq_start, seq_end = sub[5:9]
                par = si % 2
                dd = d_t2 if par else d_t
                cn = min(128, sn)
                nch = sn // cn
                DM = Dw[:, 0:512].rearrange("p (g t) -> p g t", g=4)
                DH = Dw[:, 512:1024].rearrange("p (g t) -> p g t", g=4)
                DM0 = Dw[:, 1024:1536].rearrange("p (g t) -> p g t", g=4)
                DHS = Dw[:, 1536:1792].rearrange("p (g t) -> p g t", g=4)
                for g in range(4):
                    gc = slice(g * 128, (g + 1) * 128)
                    bk = rot("bank", nbk())
                    mms = []
                    rd = ["Dw"] + [("atm", c) for c in range(nch)]
                    for c in range(nch):
                        o = ps[bk][:, c * cn:(c + 1) * cn]
                        if is_sample:
                            mms.append((o, atm[c][0:cn, gc], DM[0:cn, g, 0:cn], True, False))
                            mms.append((o, statebf[0:15, l, gc], DHS[0:15, g, :], False, True))
                            rd.append("statebf")
                        elif c == 0 and seq_start:
                            mms.append((o, atm[c][:, gc], DM0[:, g, :], True, True))
                        else:
                            mms.append((o, atm[c][:, gc], DM[:, g, :], True, False))
                            if c == 0:
                                mms.append((o, akeep[:, l, gc], DH[:, g, :], False, True))
                                rd.append(("akeep", l))
                            else:
                                mms.append((o, atm[c - 1][:, gc], DH[:, g, :], False, True))
                    mm(mms, rd, [("ps", bk)])
                    act(dd[:, g, :sn], ps[bk][:, :sn], AF.Identity, [("ps", bk)], [("d", par, g)], z=True)
                if not is_sample and not seq_end:
                    cp(akeep[:, l, :], atm[nch - 1][:, :], [("atm", nch - 1)], [("akeep", l)])

            def ya_groups(l, sub):
                si, s0, sn, xj, q = sub[:5]
                par = si % 2
                dd = d_t2 if par else d_t
                for g in range(4):
                    bk = rot("bank", nbk())
                    mm([(ps[bk][:, :sn], wpool[:, l * 4 + g, :], dd[:, g, :sn], True, True)], [("d", par, g), "wpool"],
                       [("ps", bk)], z=True)
                    act(yab[:, g, s0:s0 + sn], ps[bk][:, :sn], AF.Identity, [("ps", bk), "smallp"], [("yab", g, si)],
                        scale=col(C_PSC + l * 4 + g), z=True)

            def u_groups(sub, slot, blk, ms=(0, 1, 2, 3)):
                si, s0, sn, xj, q = sub[:5]
                for m in ms:
                    bk = rot("bank", nbk())
                    mm([(ps[bk][:, :sn], blk[:, k, m * 128:(m + 1) * 128], h[:, k, s0:s0 + sn], k == 0, k == KD - 1)
                        for k in range(KD)], [("ring", slot)] + [("h", k, si) for k in range(KD)], [("ps", bk)])
                    act(u_t[:, m, s0:s0 + sn], ps[bk][:, :sn], AF.Gelu, [("ps", bk)], [("u", m, si)], z=True)

            def v_chunks(sub, slot, blk):
                si, s0, sn, xj, q = sub[:5]
                cn = min(128, sn)
                nch = sn // cn
                vgj = []
                for c in range(nch):
                    bk = rot("bank", nbk())
                    mm([(ps[bk][0:cn, :], h[:, k, s0 + c * cn:s0 + (c + 1) * cn], blk[:, k, :], k == 0, k == KD - 1)
                        for k in range(KD)], [("ring", slot)] + [("h", k, si) for k in range(KD)], [("ps", bk)])
                    j = rot("vg", 4)
                    vgj.append(j)
                    act(vg[j][0:cn, :], ps[bk][0:cn, :], AF.Gelu, [("ps", bk)], [("vg", j)])
                    OP("dve", (lambda j_, c_, cn_: lambda e: e.bn_stats(st6[0:cn_, c_, :], vg[j_][0:cn_, :]))(j, c, cn),
                       [("vg", j)], [("st6", c)])
                    OP("dve", (lambda c_, cn_: lambda e: e.bn_aggr(mv[0:cn_, c_, :], st6[0:cn_, c_, :]))(c, cn),
                       [("st6", c)], [("mv", c)])
                return vgj

            def ln(l, is_sample, sub, vgj):
                si, s0, sn, xj, q = sub[:5]
                is_sample = sub[6]
                cn = min(128, sn)
                nch = sn // cn
                mvk = [("mv", c) for c in range(nch)]
                act(sd[0:cn, 0:nch], mv[0:cn, 0:nch, 1], AF.Sqrt, mvk, ["sd"], bias=EPS)
                OP("dve", lambda e: e.reciprocal(out=sd[0:cn, 0:nch], in_=sd[0:cn, 0:nch]), ["sd"], ["sd"])
                for c in range(nch):
                    j = vgj[c]
                    ts(n_t[0:cn, c, :], vg[j][0:cn, :], mv[0:cn, c, 0:1], sd[0:cn, c:c + 1], ALU.subtract, ALU.mult,
                       [("vg", j), ("mv", c), "sd"], [("n", c)], z=True)
                if is_sample:
                    S.op("sp", lambda e: e.dma_start(out=af32[0:DEC, :], in_=gbrow_d[l:l + 1, 0:512].partition_broadcast(DEC)),
                         writes=["af32"], sem="gbld")
                    S.op("sp", lambda e: e.dma_start(out=rs_fin[0:DEC, :], in_=gbrow_d[l:l + 1, 512:1024].partition_broadcast(DEC)),
                         writes=["rsfin"], sem="gbld2")
                    j = vgj[0]
                    ts(vg[j][0:cn, :], vg[j][0:cn, :], mv[0:cn, 0, 0:1], sd[0:cn, 0:1], ALU.subtract, ALU.mult,
                       [("vg", j), ("mv", 0), "sd"], [("vg", j)])
                    tt(vg[j][0:cn, :], vg[j][0:cn, :], af32[0:cn, :], ALU.mult, [("vg", j), "af32"], [("vg", j)])
                    tt(vg[j][0:cn, :], vg[j][0:cn, :], rs_fin[0:cn, :], ALU.add, [("vg", j), "rsfin"], [("vg", j)])
                    store_ops.append(S.op("sp", lambda e: e.dma_start(out=vout[l, :, :], in_=vg[j][0:DEC, :]),
                                          reads=[("vg", j)], sem=("vst", l)))

            def spatial(l, sub):
                si, s0, sn, xj, q = sub[:5]
                cn = min(128, sn)
                nch = sn // cn
                for hh in range(4):
                    bk = rot("bank", nbk())
                    mm([(ps[bk][:, c * cn:(c + 1) * cn], n_t[0:cn, c, hh * 128:(hh + 1) * 128], WT[0:cn, l * 4 + hh, 0:cn], True, True)
                        for c in range(nch)], [("n", c) for c in range(nch)] + ["WT"], [("ps", bk)], z=True)
                    jt = rot("tf", NTF)
                    stt(tf[jt][:, :sn].rearrange("p (c t) -> p c t", c=nch),
                        ps[bk][:, :sn].rearrange("p (c t) -> p c t", c=nch), col(C_GV + l * 4 + hh),
                        Bt[:, l * 4 + hh, 0:cn].unsqueeze(1).to_broadcast([128, nch, cn]), ALU.mult, ALU.add,
                        [("ps", bk), "smallp", ("Bt", l)], [("tf", jt)])
                    tt(yab[:, 4 + hh, s0:s0 + sn], tf[jt][:, :sn], u_t[:, hh, s0:s0 + sn], ALU.mult,
                       [("tf", jt), ("u", hh, si)], [("yab", 4 + hh, si)], z=True)

            def o_group(l, b, sub, m, slot, blk):
                si, s0, sn, xj, q = sub[:5]
                b = sub[5]
                mq = m % 4
                bk = rot("bank", nbk())
                mm([(ps[bk][:, :sn], blk[:, k, mq * 128:(mq + 1) * 128], yab[:, k, s0:s0 + sn], k == 0, k == KD - 1)
                    for k in range(KD)], [("ring", slot)] + [("yab", k, si) for k in range(KD)], [("ps", bk)], z=True)
                stt(xs[xj][:, m, :sn], ps[bk][:, :sn], mod[:, l, 16 + m, b:b + 1], xs[xj][:, m, :sn],
                    ALU.mult, ALU.add, [("ps", bk), ("mod", l, 2), ("x", xj, m)], [("x", xj, m)])

            def ff1_group(sub, fc, mq, slot, blk):
                si, s0, sn, xj, q = sub[:5]
                bk = rot("bank", nbk())
                mm([(ps[bk][:, :sn], blk[:, k, mq * 128:(mq + 1) * 128], h[:, k, s0:s0 + sn], k == 0, k == KD - 1)
                    for k in range(KD)], [("ring", slot)] + [("h", k, si) for k in range(KD)], [("ps", bk)])
                jt = rot("tf", NTF)
                act(tf[jt][:, :sn], ps[bk][:, :sn], AF.Relu, [("ps", bk)], [("tf", jt)])
                tt(hid[:, fc, s0:s0 + sn], tf[jt][:, :sn], tf[jt][:, :sn], ALU.mult, [("tf", jt)], [("hid", fc, si)], z=True)

            def ff2_group(l, b, sub, m, mq, slot, blk):
                si, s0, sn, xj, q = sub[:5]
                b = sub[5]
                bk = rot("bank", nbk())
                mm([(ps[bk][:, :sn], blk[:, fc, mq * 128:(mq + 1) * 128], hid[:, fc, s0:s0 + sn], fc == 0, fc == 15)
                    for fc in range(16)], [("ring", slot)] + [("hid", fc, si) for fc in range(16)], [("ps", bk)], z=True)
                stt(xs[xj][:, m, :sn], ps[bk][:, :sn], mod[:, l, 40 + m, b:b + 1], xs[xj][:, m, :sn],
                    ALU.mult, ALU.add, [("ps", bk), ("mod", l, 5), ("x", xj, m)], [("x", xj, m)])

            TL = [(ti, l) for ti in range(len(tiles)) for l in range(DEPTH)]
            first = tile_subs[0][0]
            n_sq(first)
            n_rest(first, 0, 1, tiles[0][0])
            pending_fin = None
            for i, (ti, l) in enumerate(TL):
                b, t0, nt, g0 = tiles[ti]
                is_sample = (b == 2)
                subs = tile_subs[ti]
                nsub = len(subs)
                sA = subs[0]
                sB = subs[1] if nsub > 1 else None
                sC = subs[2] if nsub > 2 else None

                def normC(which):
                    if sC is not None:
                        n_rest(sC, l, which, b)
                new_tile = (l == 0)
                zphase()
                if sB is not None and not (new_tile and pending_fin is not None):
                    n_sq(sB)
                if sC is not None:
                    n_sq(sC)
                bA = blk_in(l, 0)
                bU = blk_in(l, 1)
                bV = blk_in(l, 2)
                a_groups(l, b, t0, is_sample, nsub, sA, *bA)
                fin_late = None
                if new_tile and pending_fin is not None:
                    fin_stats_def(pending_fin[0])
                    fin_late = pending_fin[0]
                    pending_fin = None
                    if sB is not None:
                        n_sq(sB)
                    pooling(l, t0, is_sample, sA)
                    u_groups(sA, *bU, ms=(0, 1))
                    if sB is not None:
                        n_rest(sB, l, 1, b)
                    normC(1)
                    u_groups(sA, *bU, ms=(2, 3))
                else:
                    pooling(l, t0, is_sample, sA)
                    if sB is not None:
                        n_rest(sB, l, 1, b)
                    normC(1)
                    u_groups(sA, *bU)
                vgA = v_chunks(sA, *bV)
                ln(l, is_sample, sA, vgA)
                if sB is not None:
                    a_groups(l, b, t0, is_sample, nsub, sB, *bA)
                    if sC is None:
                        release(bA)
                    spatial(l, sA)
                    if fin_late is not None:
                        fin_chain_def(fin_late)
                    pooling(l, t0, is_sample, sB)
                    u_groups(sB, *bU)
                    ya_groups(l, sA)
                    vgB = v_chunks(sB, *bV)
                    ln(l, is_sample, sB, vgB)
                    if sC is not None:
                        a_groups(l, b, t0, is_sample, nsub, sC, *bA)
                        release(bA)
                        u_groups(sC, *bU)
                        vgC = v_chunks(sC, *bV)
                    release(bU)
                    release(bV)
                else:
                    release(bA)
                    release(bU)
                    release(bV)
                    ya_groups(l, sA)
                    spatial(l, sA)
                if ti == 0 and l == 0:
                    for _ in range(6):
                        mod_step()
                bO = [blk_out(l, 0), blk_out(l, 1)]
                for m in range(4):
                    o_group(l, b, sA, m, *bO[0])
                if sB is not None:
                    spatial(l, sB)
                if sC is not None:
                    pooling(l, t0, is_sample, sC)
                    ln(l, is_sample, sC, vgC)
                for m in range(4, 8):
                    o_group(l, b, sA, m, *bO[1])
                    if m == 5 and sB is not None:
                        ya_groups(l, sB)
                if sC is not None:
                    ya_groups(l, sC)
                    spatial(l, sC)
                n_sq(sA)
                if sB is not None:
                    for m in range(8):
                        o_group(l, b, sB, m, *bO[m // 4])
                        if m == 1:
                            n_rest(sA, l, 2, b)
                    if sC is not None:
                        for m in range(8):
                            o_group(l, b, sC, m, *bO[m // 4])
                    release(bO[0])
                    release(bO[1])
                    n_sq(sB)
                    if sC is not None:
                        n_sq(sC)
                else:
                    release(bO[0])
                    release(bO[1])
                    n_rest(sA, l, 2, b)
                for half in range(2):
                    zphase()
                    f1 = [blk_ff1(l, half, 0), blk_ff1(l, half, 1)]
                    if half == 0:
                        for fc in range(8):
                            ff1_group(sA, fc, fc % 4, *f1[fc // 4])
                            if fc == 1 and sB is not None:
                                n_rest(sB, l, 2, b)
                                normC(2)
                        if sB is not None:
                            for fc in range(8):
                                ff1_group(sB, fc, fc % 4, *f1[fc // 4])
                        if sC is not None:
                            for fc in range(8):
                                ff1_group(sC, fc, fc % 4, *f1[fc // 4])
                        release(f1[0])
                        release(f1[1])
                        if ti == 0 and l == 0:
                            mod_step()
                            mod_step()
                    else:
                        for fb in range(2):
                            for sub in subs:
                                for mq in range(4):
                                    ff1_group(sub, fb * 4 + mq, mq, *f1[fb])
                            release(f1[fb])
                        if ti == 0 and l == 0:
                            mod_step()
                            mod_step()
                    for fb in range(2, 4):
                        bf = blk_ff1(l, half, fb)
                        for sub in subs:
                            for mq in range(4):
                                ff1_group(sub, fb * 4 + mq, mq, *bf)
                        release(bf)
                        if ti == 0 and l == 0:
                            mod_step()
                    last_half = (half == 1)
                    nblk_major = 4 if not last_half else 1
                    for mb in range(nblk_major):
                        bf = blk_ff2(l, half, mb)
                        for sub in subs:
                            for mq in range(2):
                                ff2_group(l, b, sub, mb * 2 + mq, mq, *bf)
                        release(bf)
                        if ti == 0 and l == 0:
                            mod_step()
                    if last_half:
                        if ti == 0 and l == 0:
                            while mod_jobs:
                                mod_step()
                        f2 = [blk_ff2(l, half, 1), blk_ff2(l, half, 2), blk_ff2(l, half, 3)]
                        for m in range(2, 8):
                            ff2_group(l, b, sA, m, m % 2, *f2[(m - 2) // 2])
                        last_layer = (l == DEPTH - 1)
                        nxt = TL[i + 1] if i + 1 < len(TL) else None

                        def prologue():
                            if last_layer:
                                fin_rest(sA, sA[9])
                                if nxt is not None:
                                    nsub0 = tile_subs[nxt[0]][0]
                                    n_sq(nsub0)
                                    n_rest(nsub0, 0, 1, tiles[nxt[0]][0])
                            else:
                                n_rest(sA, l + 1, 1, b)

                        n_sq(sA)
                        if sB is not None:
                            for m in range(2, 8):
                                ff2_group(l, b, sB, m, m % 2, *f2[(m - 2) // 2])
                                if m % 2 == 1 and sC is None:
                                    release(f2[(m - 2) // 2])
                                if m == 3:
                                    prologue()
                            if sC is not None:
                                for m in range(2, 8):
                                    ff2_group(l, b, sC, m, m % 2, *f2[(m - 2) // 2])
                                for fb_ in f2:
                                    release(fb_)
                            if last_layer:
                                n_sq(sB)
                                if nxt is not None:
                                    pending_fin = (sB, sB[9])
                                else:
                                    fin_rest(sB, sB[9])
                                    if sC is not None:
                                        n_sq(sC)
                                        fin_rest(sC, sC[9])
                        else:
                            for fb_ in f2:
                                release(fb_)
                            prologue()
            S.op("sp", None, after=store_ops)
            return collected

        plan = record(Sched(), None)
        S = Sched()
        record(S, plan)
        S.emit(nc)
    return nc


def _fm(v, nchunk):
    return np.ascontiguousarray(v.reshape(nchunk, 128).T)


def kernel(x_prompt, x_sample, state_pool, c_prompt, c_sample, w_ada, b_ada, norm_mix_g, w_in,
           w_pool, pool_scale, v_norm_g, v_norm_b, w_spatial, b_spatial, w_out, norm_ffn_g,
           w_ff1, w_ff2, final_norm_g):
    f32 = np.float32
    A = lambda a: np.ascontiguousarray(np.asarray(a, dtype=f32))
    x_prompt, x_sample, state_pool = A(x_prompt), A(x_sample), A(state_pool)
    c_prompt, c_sample = A(c_prompt), A(c_sample)
    w_ada, w_in, w_out, w_ff1, w_ff2 = A(w_ada), A(w_in), A(w_out), A(w_ff1), A(w_ff2)
    b_ada, norm_mix_g, norm_ffn_g, final_norm_g = A(b_ada), A(norm_mix_g), A(norm_ffn_g), A(final_norm_g)
    w_pool, pool_scale, v_norm_g, v_norm_b = A(w_pool), A(pool_scale), A(v_norm_g), A(v_norm_b)
    w_spatial, b_spatial = A(w_spatial), A(b_spatial)

    shared_small = np.zeros((128, NSMALL), f32)
    for l in range(DEPTH):
        shared_small[:, C_BADA + l * 48:C_BADA + (l + 1) * 48] = _fm(b_ada[l], 48)
        shared_small[:, C_GMIX + l * 8:C_GMIX + (l + 1) * 8] = _fm(norm_mix_g[l], 8)
        shared_small[:, C_GFFN + l * 8:C_GFFN + (l + 1) * 8] = _fm(norm_ffn_g[l], 8)
        shared_small[:, C_PSC + l * 4:C_PSC + (l + 1) * 4] = _fm(pool_scale[l], 4)
        shared_small[:, C_GV + l * 4:C_GV + (l + 1) * 4] = _fm(v_norm_g[l], 4)
        shared_small[:, C_BV + l * 4:C_BV + (l + 1) * 4] = _fm(v_norm_b[l], 4)
    shared_small[:, C_GFIN:C_GFIN + 8] = _fm(final_norm_g, 8)
    for g, w in enumerate(POOL_W):
        for p in range(16):
            shared_small[:, C_INV + g * 16 + p] = 1.0 / min(p + 1, w)
    bsrow = np.ascontiguousarray(b_spatial.reshape(DEPTH, 512))
    gbrow = np.ascontiguousarray(np.concatenate([v_norm_g, v_norm_b], axis=1))
    maskt = np.ascontiguousarray(np.triu(np.ones((128, 128), f32)))
    wspT = np.ascontiguousarray(w_spatial.transpose(3, 0, 1, 2).reshape(128, DEPTH * 4 * 128))
    wpool_l = np.ascontiguousarray(w_pool.transpose(2, 0, 1, 3).reshape(128, DEPTH * 4 * 128))

    tt_ = np.arange(128)[None, :]
    ss_ = np.arange(128)[:, None]
    dwin = np.zeros((128, 3 * 4 * 128 + 4 * 64), f32)
    for g, w in enumerate(POOL_W):
        eye = (ss_ == tt_).astype(f32)
        dm = ((ss_ <= tt_) & (ss_ > tt_ - w)).astype(f32) / w - eye
        dh = (ss_ > tt_ - w + 128).astype(f32) / w
        cnt = np.minimum(tt_ + 1, w).astype(f32)
        dm0 = ((ss_ <= tt_) & (ss_ > tt_ - w)).astype(f32) / cnt - eye
        dhs = (ss_[:, :] > tt_[:, :64] - w + 15).astype(f32) / w
        dhs[15:, :] = 0.0
        dwin[:, 0 + g * 128:0 + (g + 1) * 128] = dm
        dwin[:, 512 + g * 128:512 + (g + 1) * 128] = dh
        dwin[:, 1024 + g * 128:1024 + (g + 1) * 128] = dm0
        dwin[:, 1536 + g * 64:1536 + (g + 1) * 64] = dhs

    in_maps = []
    for i in range(NCORES):
        xt = np.concatenate([x_prompt[2 * i], x_prompt[2 * i + 1], x_sample[i]], axis=0)
        x_fm = np.ascontiguousarray(xt.reshape(NTOK, KD, 128).transpose(2, 1, 0))
        sp = shared_small.copy()
        cs = np.stack([c_prompt[2 * i], c_prompt[2 * i + 1], c_sample[i]], axis=0)
        sp[:, C_C:C_C + 24] = cs.reshape(3, KD, 128).transpose(2, 1, 0).reshape(128, 24)
        in_maps.append({
            "x_fm": x_fm, "smallp": sp, "bsrow": bsrow, "gbrow": gbrow, "maskt": maskt, "wspT": wspT,
            "wpool_l": wpool_l, "statein": np.ascontiguousarray(state_pool[:, i]), "dwin": dwin,
            "w_ada": w_ada, "w_in": w_in, "w_out": w_out, "w_ff1": w_ff1, "w_ff2": w_ff2,
        })
    nc = build_nc()
    res = run_bass_kernel_spmd(nc, in_maps, core_ids=list(range(NCORES)))

    y_prompt = np.empty((16, SEQ, D), f32)
    y_sample = np.empty((8, DEC, D), f32)
    sp_prompt = np.empty((DEPTH, 16, 15, 512), f32)
    sp_sample = np.empty((DEPTH, 8, 15, 512), f32)
    sv = np.empty((DEPTH, 8, DEC, 512), f32)
    for i in range(NCORES):
        r = res.results[i]
        yt = np.asarray(r["y_fm"]).transpose(2, 1, 0).reshape(NTOK, D)
        y_prompt[2 * i] = yt[0:SEQ]
        y_prompt[2 * i + 1] = yt[SEQ:2 * SEQ]
        y_sample[i] = yt[2 * SEQ:]
        po = np.asarray(r["pool_out"])
        sp_prompt[:, 2 * i] = po[0]
        sp_prompt[:, 2 * i + 1] = po[1]
        sp_sample[:, i] = po[2]
        sv[:, i] = np.asarray(r["vout"])
    return (y_prompt, y_sample, sp_prompt, sp_sample, sv)
```
